# Optimizing a Trainium2 kernel written in Bass

```python
import functools
import jax, jax.numpy as jnp
from jax import lax
import numpy as np

D_MODEL = 1024
BATCH = 2
SEQ = 8192
DEPTH = 4
DEC_BATCH = 4
DEC_SEQ = 8192
PAST_LEN = 128

BRANCH_WIDTH = 512
N_BRANCH = 3
A_HEADS = 4
A_HEAD_K = 128
A_HEAD_V = BRANCH_WIDTH // A_HEADS
A_KEY = A_HEADS * A_HEAD_K
A_VAL = BRANCH_WIDTH
B_HEADS = 4
B_HEAD_V = BRANCH_WIDTH // B_HEADS
B_HEAD_K = B_HEAD_V // 2
B_KEY = B_HEADS * B_HEAD_K
B_VAL = BRANCH_WIDTH
GLA_RANK = 16
GLA_NORMALIZER = 16.0
C_WIDTH = BRANCH_WIDTH
C_BLOCKS = 8
C_BLOCK = C_WIDTH // C_BLOCKS
C_CONV = 4
RG_C = 8.0
D_FF = 2816
FFN_CONV = 3
CHUNK = 16
EPS = 1e-6

IN_SPLITS = (A_KEY, A_KEY, A_KEY, A_VAL, A_VAL,
             B_KEY, B_KEY, B_VAL, B_VAL, GLA_RANK, GLA_RANK,
             C_WIDTH, C_WIDTH,
             D_MODEL, D_MODEL, D_MODEL)
N_IN = sum(IN_SPLITS)

kernel_name = 'hybrid_hgrn2_gla_rglru_encoder'


def rms_norm(x, w):
    xf = x.astype(jnp.float32)
    y = xf * lax.rsqrt(jnp.mean(xf * xf, axis=-1, keepdims=True) + EPS)
    return (y * w.astype(jnp.float32)).astype(x.dtype)


def dw_conv(x, w, b, left):
    K = w.shape[0]
    T = x.shape[1]
    xp = jnp.pad(x, ((0, 0), (left, K - 1 - left), (0, 0)))
    out = b
    for j in range(K):
        out = out + xp[:, j:j + T] * w[j]
    return out


def gla_chunk(q, k, v, log_g):
    B, T, H, dk = q.shape
    n = T // CHUNK

    def blk(t):
        return t.astype(jnp.float32).reshape(B, n, CHUNK, H, t.shape[-1])

    q, k, v, log_g = blk(q) * (dk ** -0.5), blk(k), blk(v), blk(log_g)
    b = jnp.cumsum(log_g, axis=2)
    b_last = b[:, :, -1]
    q_dec = q * jnp.exp(b)
    att = jnp.einsum('bnihd,bnjhd->bnhij', q_dec, k * jnp.exp(-b))
    att = jnp.where(jnp.tril(jnp.ones((CHUNK, CHUNK), dtype=bool)), att, 0.0)
    o_intra = jnp.einsum('bnhij,bnjhe->bnihe', att, v)
    k_tail = k * jnp.exp(b_last[:, :, None] - b)

    def step(S, xs):
        qd, kt, vc, bl = xs
        o = jnp.einsum('bihd,bhde->bihe', qd, S)
        S = S * jnp.exp(bl)[..., None] + jnp.einsum('bjhd,bjhe->bhde', kt, vc)
        return S, o

    S0 = jnp.zeros((B, H, dk, v.shape[-1]), jnp.float32)
    xs = tuple(jnp.moveaxis(t, 1, 0) for t in (q_dec, k_tail, v, b_last))
    _, o_inter = lax.scan(step, S0, xs)
    return (o_intra + jnp.moveaxis(o_inter, 0, 1)).reshape(B, T, H, -1)


def bidir_gla(q, k_f, k_b, v, lg_f, lg_b):
    flip = lambda t: jnp.flip(t, axis=1)
    return gla_chunk(q, k_f, v, lg_f) + flip(gla_chunk(flip(q), flip(k_b), flip(v), flip(lg_b)))


def _lin_combine(left, right):
    a1, b1 = left
    a2, b2 = right
    return a1 * a2, a2 * b1 + b2


def rglru(x, wa, ba, wx, bx, lam, reverse):
    B, T, C = x.shape
    xb = x.reshape(B, T, C_BLOCKS, C_BLOCK)
    r = jax.nn.sigmoid(jnp.einsum('btnk,nkj->btnj', xb, wa).reshape(B, T, C) + ba)
    i = jax.nn.sigmoid(jnp.einsum('btnk,nkj->btnj', xb, wx).reshape(B, T, C) + bx)
    log_a = -RG_C * r * jax.nn.softplus(-lam)
    a = jnp.exp(log_a)
    u = jnp.sqrt(-jnp.expm1(2.0 * log_a)) * (i * x)
    _, h = lax.associative_scan(_lin_combine, (a, u), axis=1, reverse=reverse)
    return h


def token_mixers(h, w_in, lb, hgrn_norm_w, gla_up_w, gla_up_b, gla_norm_w,
                 c_conv_w, c_conv_b, rglru_wa, rglru_ba, rglru_wx, rglru_bx, rglru_lam,
                 w_branch, w_out):
    B, T, _ = h.shape
    f32 = lambda t: t.astype(jnp.float32)
    heads = lambda t, n: t.reshape(B, T, n, -1)
    parts = jnp.split(h @ w_in, np.cumsum(IN_SPLITS)[:-1].tolist(), axis=-1)
    (q_a, zf_a, zb_a, i_a, og_a, q_b, k_b, v_b, og_b, lrf_b, lrb_b,
     x_c, y_c, g_a, g_b, g_c) = parts

    lbf = f32(lb)
    lg_af = jnp.log(lbf[0] + (1.0 - lbf[0]) * jax.nn.sigmoid(f32(zf_a)))
    lg_ab = jnp.log(lbf[1] + (1.0 - lbf[1]) * jax.nn.sigmoid(f32(zb_a)))
    o_a = bidir_gla(heads(jax.nn.silu(f32(q_a)), A_HEADS),
                    heads(-jnp.expm1(lg_af), A_HEADS), heads(-jnp.expm1(lg_ab), A_HEADS),
                    heads(f32(i_a), A_HEADS),
                    heads(lg_af, A_HEADS), heads(lg_ab, A_HEADS))
    o_a = (rms_norm(o_a, hgrn_norm_w).reshape(B, T, A_VAL) * jax.nn.silu(f32(og_a))).astype(h.dtype)

    up_w, up_b = f32(gla_up_w), f32(gla_up_b)
    lg_bf = jax.nn.log_sigmoid(f32(lrf_b) @ up_w[0] + up_b[0]) / GLA_NORMALIZER
    lg_bb = jax.nn.log_sigmoid(f32(lrb_b) @ up_w[1] + up_b[1]) / GLA_NORMALIZER
    kb = heads(f32(k_b), B_HEADS)
    o_b = bidir_gla(heads(f32(q_b), B_HEADS), kb, kb, heads(f32(v_b), B_HEADS),
                    heads(lg_bf, B_HEADS), heads(lg_bb, B_HEADS))
    o_b = (rms_norm(o_b, gla_norm_w).reshape(B, T, B_VAL) * jax.nn.silu(f32(og_b))).astype(h.dtype)

    xc = f32(dw_conv(x_c, c_conv_w, c_conv_b, C_CONV // 2))
    hc = (rglru(xc, rglru_wa[0], rglru_ba[0], rglru_wx[0], rglru_bx[0], rglru_lam[0], False)
          + rglru(xc, rglru_wa[1], rglru_ba[1], rglru_wx[1], rglru_bx[1], rglru_lam[1], True))
    o_c = (hc * jax.nn.gelu(f32(y_c))).astype(h.dtype)

    m = (jax.nn.sigmoid(g_a) * (o_a @ w_branch[0])
         + jax.nn.sigmoid(g_b) * (o_b @ w_branch[1])
         + jax.nn.sigmoid(g_c) * (o_c @ w_branch[2]))
    return m @ w_out


def conv_ffn(h, ffn_up, ffn_conv_w, ffn_conv_b, ffn_down):
    u = dw_conv(h @ ffn_up, ffn_conv_w, ffn_conv_b, FFN_CONV // 2)
    gate, val = jnp.split(u, 2, axis=-1)
    return (jax.nn.gelu(gate) * val) @ ffn_down


def trunk(x, norm_mix_w, w_in, hgrn_lb_logits, hgrn_norm_w, gla_up_w, gla_up_b, gla_norm_w,
          c_conv_w, c_conv_b, rglru_wa, rglru_ba, rglru_wx, rglru_bx, rglru_lam,
          w_branch, w_out, norm_ffn_w, ffn_up, ffn_conv_w, ffn_conv_b, ffn_down, final_norm_w):
    lb_all = jnp.cumsum(jax.nn.softmax(hgrn_lb_logits.astype(jnp.float32), axis=0), axis=0)
    lb_all = lb_all - lb_all[0]
    for l in range(DEPTH):
        h = rms_norm(x, norm_mix_w[l])
        x = x + token_mixers(h, w_in[l], lb_all[l], hgrn_norm_w[l], gla_up_w[l], gla_up_b[l],
                             gla_norm_w[l], c_conv_w[l], c_conv_b[l], rglru_wa[l], rglru_ba[l],
                             rglru_wx[l], rglru_bx[l], rglru_lam[l], w_branch[l], w_out[l]).astype(x.dtype)
        h = rms_norm(x, norm_ffn_w[l])
        x = x + conv_ffn(h, ffn_up[l], ffn_conv_w[l], ffn_conv_b[l], ffn_down[l]).astype(x.dtype)
    return rms_norm(x, final_norm_w)


def setup_inputs(seed: int = 0) -> dict:
    key = jax.random.key(seed)
    ks = jax.random.split(key, 26)
    nrm = lambda k, shape, scale: jax.random.normal(k, shape, jnp.float32) * scale
    gain = lambda k, shape: 1.0 + 0.05 * jax.random.normal(k, shape, jnp.float32)
    u = jax.random.uniform(ks[15], (DEPTH, 2, C_WIDTH), jnp.float32, minval=0.9, maxval=0.999)
    p = u ** (1.0 / RG_C)
    lam = jnp.log(p) - jnp.log1p(-p)
    return {
        'x_prompt': nrm(ks[0], (BATCH, SEQ, D_MODEL), 1.0),
        'x_sample': nrm(ks[1], (DEC_BATCH, DEC_SEQ, D_MODEL), 1.0),
        'norm_mix_w': gain(ks[2], (DEPTH, D_MODEL)),
        'w_in': nrm(ks[3], (DEPTH, D_MODEL, N_IN), D_MODEL ** -0.5),
        'hgrn_lb_logits': nrm(ks[4], (DEPTH, 2, A_KEY), 0.5),
        'hgrn_norm_w': gain(ks[5], (DEPTH, A_HEAD_V)),
        'gla_up_w': nrm(ks[6], (DEPTH, 2, GLA_RANK, B_KEY), GLA_RANK ** -0.5),
        'gla_up_b': nrm(ks[7], (DEPTH, 2, B_KEY), 0.1),
        'gla_norm_w': gain(ks[8], (DEPTH, B_HEAD_V)),
        'c_conv_w': nrm(ks[9], (DEPTH, C_CONV, C_WIDTH), C_CONV ** -0.5),
        'c_conv_b': nrm(ks[10], (DEPTH, C_WIDTH), 0.02),
        'rglru_wa': nrm(ks[11], (DEPTH, 2, C_BLOCKS, C_BLOCK, C_BLOCK), C_BLOCK ** -0.5),
        'rglru_ba': nrm(ks[12], (DEPTH, 2, C_WIDTH), 0.1),
        'rglru_wx': nrm(ks[13], (DEPTH, 2, C_BLOCKS, C_BLOCK, C_BLOCK), C_BLOCK ** -0.5),
        'rglru_bx': nrm(ks[14], (DEPTH, 2, C_WIDTH), 0.1),
        'rglru_lam': lam,
        'w_branch': nrm(ks[16], (DEPTH, N_BRANCH, BRANCH_WIDTH, D_MODEL), BRANCH_WIDTH ** -0.5),
        'w_out': nrm(ks[17], (DEPTH, D_MODEL, D_MODEL), D_MODEL ** -0.5),
        'norm_ffn_w': gain(ks[18], (DEPTH, D_MODEL)),
        'ffn_up': nrm(ks[19], (DEPTH, D_MODEL, 2 * D_FF), D_MODEL ** -0.5),
        'ffn_conv_w': nrm(ks[20], (DEPTH, FFN_CONV, 2 * D_FF), FFN_CONV ** -0.5),
        'ffn_conv_b': nrm(ks[21], (DEPTH, 2 * D_FF), 0.02),
        'ffn_down': nrm(ks[22], (DEPTH, D_FF, D_MODEL), D_FF ** -0.5),
        'final_norm_w': gain(ks[23], (D_MODEL,)),
    }


def reference(x_prompt, x_sample, norm_mix_w, w_in, hgrn_lb_logits, hgrn_norm_w, gla_up_w, gla_up_b,
              gla_norm_w, c_conv_w, c_conv_b, rglru_wa, rglru_ba, rglru_wx, rglru_bx, rglru_lam,
              w_branch, w_out, norm_ffn_w, ffn_up, ffn_conv_w, ffn_conv_b, ffn_down, final_norm_w):
    run = functools.partial(
        trunk, norm_mix_w=norm_mix_w, w_in=w_in, hgrn_lb_logits=hgrn_lb_logits, hgrn_norm_w=hgrn_norm_w,
        gla_up_w=gla_up_w, gla_up_b=gla_up_b, gla_norm_w=gla_norm_w, c_conv_w=c_conv_w, c_conv_b=c_conv_b,
        rglru_wa=rglru_wa, rglru_ba=rglru_ba, rglru_wx=rglru_wx, rglru_bx=rglru_bx, rglru_lam=rglru_lam,
        w_branch=w_branch, w_out=w_out, norm_ffn_w=norm_ffn_w, ffn_up=ffn_up, ffn_conv_w=ffn_conv_w,
        ffn_conv_b=ffn_conv_b, ffn_down=ffn_down, final_norm_w=final_norm_w)
    y_prompt = run(x_prompt)
    y_sample = run(x_sample)
    return (y_prompt, y_sample)
```

```python
import numpy as np
import concourse.bass as bass
import concourse.mybir as mybir
from concourse.bass_utils import run_bass_kernel_spmd

F32 = mybir.dt.float32
BF16 = mybir.dt.bfloat16
AF = mybir.ActivationFunctionType
ALU = mybir.AluOpType

D = 1024
NIN = 8224
DFF = 2816
TT = 512
import os
INTERLEAVE = not os.environ.get('NOINT')
BCL = 128
EPS = 1e-6
C_QA, C_ZF, C_ZB, C_IA, C_OGA = 0, 512, 1024, 1536, 2048
C_QB, C_KB, C_VB, C_OGB, C_LRF, C_LRB = 2560, 2816, 3072, 3584, 4096, 4112
C_XC, C_YC, C_GA = 4128, 4640, 5152


class Sched:
    def __init__(self, nc):
        self.nc = nc
        self.ops = []
        self.engs = {'pe': nc.tensor, 'act': nc.scalar, 'dve': nc.vector, 'pool': nc.gpsimd, 'sp': nc.sync}

    def op(self, eng, fn, reads=(), writes=()):
        self.ops.append((eng, fn, tuple(reads), tuple(writes), None))

    def dma(self, q, fn, reads=(), writes=(), key=None):
        self.ops.append((q, fn, tuple(reads), tuple(writes), key))

    def emit(self):
        nc = self.nc
        ops = self.ops
        n = len(ops)
        writers, readers = {}, {}
        deps = [None] * n
        signal = [False] * n
        for i, (eng, fn, reads, writes, key) in enumerate(ops):
            stream = ('dma', key) if key is not None else eng
            d = set()
            for r in reads:
                w = writers.get(r)
                if w:
                    d.update(w.values())
            for wr in writes:
                w = writers.get(wr)
                if w:
                    d.update(w.values())
                rd = readers.get(wr)
                if rd:
                    d.update(rd.values())
            dl = []
            for j in d:
                if ops[j][4] is None and ops[j][0] == 'pe' and eng == 'pe' and key is None:
                    continue
                dl.append(j)
                signal[j] = True
            deps[i] = dl
            for r in reads:
                readers.setdefault(r, {})[stream] = i
            for wr in writes:
                writers.setdefault(wr, {})[stream] = i
        sems = {}

        def sem_for(name):
            if name not in sems:
                sems[name] = nc.alloc_semaphore('s_%d' % len(sems))
            return sems[name]
        count = {}
        sigval = [None] * n
        seen = {}
        nmap = {'q_act': 'act', 'q_pool': 'pool'}
        for i, (eng, fn, reads, writes, key) in enumerate(ops):
            ename = nmap.get(eng, eng)
            e = self.engs[ename]
            sn = seen.setdefault(ename, {})
            need = {}
            for j in deps[i]:
                s, v = sigval[j]
                if need.get(s, 0) < v:
                    need[s] = v
            for s, v in need.items():
                if sn.get(s, 0) >= v:
                    continue
                e.wait_ge(sem_for(s), v)
                sn[s] = v
            if key is not None:
                s = ('dma', key)
                pv = count.get(s, 0)
                if pv and sn.get(s, 0) < pv:
                    e.wait_ge(sem_for(s), pv)
                    sn[s] = pv
            inst = fn(e)
            if key is not None:
                s = ('dma', key)
                count[s] = count.get(s, 0) + 16
                inst.then_inc(sem_for(s), 16)
                sigval[i] = (s, count[s])
            elif signal[i]:
                s = ename
                count[s] = count.get(s, 0) + 1
                inst.then_inc(sem_for(s), 1)
                sigval[i] = (s, count[s])
        sp = self.engs['sp']
        for s, v in count.items():
            if isinstance(s, tuple):
                sp.wait_ge(sem_for(s), v)
        return dict(n_ops=n, n_sems=len(sems))


class V:
    def __init__(self, ap, keys):
        self.ap = ap
        self.keys = list(keys)


class Buf:
    def __init__(self, nc, name, shape, dtype):
        self.t = nc.alloc_sbuf_tensor(name, list(shape), dtype)
        self.name = name
        self.shape = shape

    def g(self, i, c0=None, c1=None):
        return V(self.t[:, i, c0:c1], [(self.name, i)])

    def gp(self, i, p, c0=None, c1=None):
        return V(self.t[0:p, i, c0:c1], [(self.name, i)])

    def all(self):
        n = self.shape[1] if len(self.shape) > 2 else 1
        return V(self.t[:], [(self.name, i) for i in range(n)])

    def v(self, ap_fn, idxs):
        return V(ap_fn(self.t), [(self.name, i) for i in idxs])


def build(T, L, n_mix=3):
    assert T % TT == 0
    NT = T // TT
    XW = T + 512
    NWIN = (T + 509) // 510
    nc = bass.Bass("TRN2", target_bir_lowering=False)
    S = Sched(nc)

    def din(name, shape):
        return nc.dram_tensor(name, list(shape), F32, kind="ExternalInput").ap()
    x_in = din("x", [T, D])
    norm_mix_w = din("norm_mix_w", [L, D])
    w_in = din("w_in", [L, D, NIN])
    lb_logits = din("hgrn_lb_logits", [4, 2, 512])
    hgrn_norm_w = din("hgrn_norm_w", [L, 128])
    gla_up_w = din("gla_up_w", [L, 2, 16, 256])
    gla_up_b = din("gla_up_b", [L, 2, 256])
    gla_norm_w = din("gla_norm_w", [L, 128])
    c_conv_w = din("c_conv_w", [L, 4, 512])
    c_conv_b = din("c_conv_b", [L, 512])
    rglru_wa = din("rglru_wa", [L, 2, 8, 64, 64])
    rglru_ba = din("rglru_ba", [L, 2, 512])
    rglru_wx = din("rglru_wx", [L, 2, 8, 64, 64])
    rglru_bx = din("rglru_bx", [L, 2, 512])
    rglru_lam = din("rglru_lam", [L, 2, 512])
    w_branch = din("w_branch", [L, 3, 512, D])
    w_out = din("w_out", [L, D, D])
    norm_ffn_w = din("norm_ffn_w", [L, D])
    ffn_up = din("ffn_up", [L, D, 2 * DFF])
    ffn_conv_w = din("ffn_conv_w", [L, 3, 2 * DFF])
    ffn_conv_b = din("ffn_conv_b", [L, 2 * DFF])
    ffn_down = din("ffn_down", [L, DFF, D])
    final_norm_w = din("final_norm_w", [D])
    y_out = nc.dram_tensor("y", [T, D], F32, kind="ExternalOutput").ap()

    def dscr(name, shape, dt):
        return nc.dram_tensor(name, list(shape), dt, kind="Internal").ap()
    xA = dscr("xA", [D, XW], F32)
    xB = dscr("xB", [D, XW], F32)
    obwd = dscr("obwd", [1536, T], F32)
    wb_in = dscr("wb_in", [L, D, NIN], BF16)
    wb_br = dscr("wb_br", [L, 1536, D], BF16)
    wb_out = dscr("wb_out", [L, D, D], BF16)
    wb_up = dscr("wb_up", [L, D, 2 * DFF], BF16)
    wb_dn = dscr("wb_dn", [L, DFF, D], BF16)

    ps = [nc.alloc_psum_tensor("ps%d" % i, [128, 512], F32) for i in range(7)]
    psT = nc.alloc_psum_tensor("psT", [128, 1024], BF16)

    def P(i, c0=None, c1=None, p=128):
        return V(ps[i][0:p, c0:c1], [('ps', i)])

    xt = Buf(nc, "xt", [128, 8, 516], F32)
    sqb = Buf(nc, "sqb", [128, 8, 516], BF16)
    hb = Buf(nc, "hb", [128, 8, 516], BF16)
    rstd = Buf(nc, "rstd", [128, 1, 516], F32)
    NWS = 3
    wsl = [Buf(nc, "wsl%d" % i, [128, 8, 512], BF16) for i in range(NWS)]
    tmp = [Buf(nc, "tmp%d" % i, [128, 1, 516], F32) for i in range(11)]
    qd = Buf(nc, "qd", [128, 4, 512], BF16)
    ki = Buf(nc, "ki", [128, 4, 512], BF16)
    ktl = Buf(nc, "ktl", [128, 4, 512], BF16)
    ktm = [Buf(nc, "ktm%d" % i, [128, 4, 512], BF16) for i in range(2)]
    Vt = Buf(nc, "Vt", [128, 4, 512], BF16)
    qdB = Buf(nc, "qdB", [128, 4, 512], BF16)
    kiB = Buf(nc, "kiB", [128, 4, 512], BF16)
    ktlB = Buf(nc, "ktlB", [128, 4, 512], BF16)
    VtB = Buf(nc, "VtB", [128, 4, 512], BF16)
    lamB = Buf(nc, "lamB", [128, 4, 16], F32)
    attM = [Buf(nc, "attM%d" % i, [128, 1, 512], BF16) for i in range(2)]
    lam = Buf(nc, "lam", [128, 4, 16], F32)
    Scur = [Buf(nc, "Scur%d" % i, [128, 4, 128], F32) for i in range(2)]
    Sbf = [Buf(nc, "Sbf%d" % i, [128, 4, 640], BF16) for i in range(2)]
    obuf = [Buf(nc, "obuf%d" % i, [128, 4, 512], F32) for i in range(3)]
    obf = [Buf(nc, "obf%d" % i, [128, 4, 512], BF16) for i in range(3)]
    xcb = Buf(nc, "xcb", [128, 4, 512], BF16)
    hst = Buf(nc, "hst", [128, 4, 2], F32)
    mbf = Buf(nc, "mbf", [128, 8, 512], BF16)
    wbd = Buf(nc, "wbd", [128, 16, 128], BF16)
    upw = Buf(nc, "upw", [16, 2, 256], BF16)
    upwf = Buf(nc, "upwf", [16, 2, 256], F32)
    lrb = Buf(nc, "lrb", [16, 1, 512], BF16)
    NCST = 1600
    cst = Buf(nc, "cst", [128, 1, NCST], F32)
    identf = Buf(nc, "identf", [128, 1, 128], F32)
    identb = Buf(nc, "identb", [128, 1, 128], BF16)
    onesb = Buf(nc, "onesb", [128, 1, 128], BF16)
    zerob = Buf(nc, "zerob", [128, 1, 512], BF16)
    maskF = Buf(nc, "maskF", [128, 1, 128], F32)
    maskB = Buf(nc, "maskB", [128, 1, 128], F32)
    cmask = Buf(nc, "cmask", [128, 1, 4], F32)
    maskF1 = Buf(nc, "maskF1", [128, 1, 128], F32)
    maskB1 = Buf(nc, "maskB1", [128, 1, 128], F32)
    rmF = Buf(nc, "rmF", [128, 1, 512], F32)
    rmB = Buf(nc, "rmB", [128, 1, 512], F32)

    def actg(j, c0=None, c1=None):
        if j < 12:
            return obf[j // 4].g(j % 4, c0, c1)
        if j < 20:
            return mbf.g(j - 12, c0, c1)
        return xcb.g(j - 20, c0, c1)

    def k_(*vs):
        r = []
        for v in vs:
            if isinstance(v, V):
                r += v.keys
        return r

    def act_(out, in_, func, scale=1.0, bias=0.0):
        sc = scale.ap if isinstance(scale, V) else scale
        bi = bias.ap if isinstance(bias, V) else bias
        S.op('act', lambda e: e.activation(out=out.ap, in_=in_.ap, func=func, bias=bi, scale=sc),
             reads=k_(in_, scale, bias), writes=out.keys)

    def tt(eng, out, a, b, op):
        S.op(eng, lambda e: e.tensor_tensor(out=out.ap, in0=a.ap, in1=b.ap, op=op), reads=k_(a, b), writes=out.keys)

    def ts(eng, out, a, s1, s2, op0, op1=None):
        s1a = s1.ap if isinstance(s1, V) else s1
        s2a = s2.ap if isinstance(s2, V) else s2
        if op1 is None:
            S.op(eng, lambda e: e.tensor_scalar(out=out.ap, in0=a.ap, scalar1=s1a, scalar2=None, op0=op0),
                 reads=k_(a, s1), writes=out.keys)
        else:
            S.op(eng, lambda e: e.tensor_scalar(out=out.ap, in0=a.ap, scalar1=s1a, scalar2=s2a, op0=op0, op1=op1),
                 reads=k_(a, s1, s2), writes=out.keys)

    def stt(eng, out, a, sc, b, op0, op1):
        sca = sc.ap if isinstance(sc, V) else sc
        S.op(eng, lambda e: e.scalar_tensor_tensor(out=out.ap, in0=a.ap, scalar=sca, in1=b.ap, op0=op0, op1=op1),
             reads=k_(a, sc, b), writes=out.keys)

    def cp(eng, out, a):
        if eng == 'act':
            act_(out, a, AF.Copy)
        else:
            S.op(eng, lambda e: e.tensor_copy(out=out.ap, in_=a.ap), reads=a.keys, writes=out.keys)

    def rcp(out, a):
        S.op('dve', lambda e: e.reciprocal(out=out.ap, in_=a.ap), reads=a.keys, writes=out.keys)

    def scan(out, d0, d1, init):
        ia = init.ap if isinstance(init, V) else init
        S.op('dve', lambda e: e.tensor_tensor_scan(out=out.ap, data0=d0.ap, data1=d1.ap, initial=ia,
                                                    op0=ALU.mult, op1=ALU.add),
             reads=k_(d0, d1, init), writes=out.keys)

    def mm(out, lhsT, rhs, start, stop):
        S.op('pe', lambda e: e.matmul(out.ap, lhsT=lhsT.ap, rhs=rhs.ap, start=start, stop=stop, skip_group_check=True),
             reads=k_(lhsT, rhs), writes=out.keys)

    def tr(out, in_, ident):
        S.op('pe', lambda e: e.transpose(out=out.ap, in_=in_.ap, identity=ident.ap), reads=k_(in_, ident), writes=out.keys)

    def memset(eng, out, val):
        S.op(eng, lambda e: e.memset(out.ap, val), writes=out.keys)

    def dma(q, out, in_, key, slow=False):
        S.dma(q, lambda e: e.dma_start(out=out.ap, in_=in_.ap, allow_slow_non_contiguous=slow),
              reads=in_.keys, writes=out.keys, key=key)

    def asel(out, pattern, cmp, base, cm):
        S.op('pool', lambda e: e.affine_select(out=out.ap, in_=out.ap, pattern=pattern, compare_op=cmp, fill=0.0,
                                               base=base, channel_multiplier=cm), reads=out.keys, writes=out.keys)

    memset('pool', identf.all(), 1.0)
    asel(identf.all(), [[-1, 128]], ALU.is_equal, 0, 1)
    cp('dve', identb.all(), identf.all())
    memset('pool', onesb.all(), 1.0)
    memset('pool', zerob.all(), 0.0)
    memset('pool', tmp[10].all(), 0.0)
    for mk, pat, cm in ((maskF, [[1, 128]], -1), (maskB, [[-1, 128]], 1)):
        memset('pool', mk.all(), 1.0)
        asel(mk.all(), pat, ALU.is_ge, 0, cm)
        for c in range(4):
            blk = mk.g(0, 32 * c, 32 * c + 32)
            asel(blk, [[0, 32]], ALU.is_ge, -32 * c, 1)
            asel(blk, [[0, 32]], ALU.is_ge, 32 * c + 31, -1)
    for mk, pat, cm in ((maskF1, [[1, 128]], -1), (maskB1, [[-1, 128]], 1)):
        memset('pool', mk.all(), 1.0)
        asel(mk.all(), pat, ALU.is_ge, 0, cm)
    memset('pool', cmask.all(), 1.0)
    for c in range(4):
        blk = cmask.g(0, c, c + 1)
        asel(blk, [[0, 1]], ALU.is_ge, -32 * c, 1)
        asel(blk, [[0, 1]], ALU.is_ge, 32 * c + 31, -1)
    memset('pool', rmF.all(), 1.0)
    memset('pool', rmB.all(), 1.0)
    memset('pool', rmF.v(lambda t: t[:, 0, 0:512:32], [0]), 0.0)
    memset('pool', rmB.v(lambda t: t[:, 0, 31:512:32], [0]), 0.0)

    ccol = [0]
    cmap = {}

    def newcols(name, n):
        c0 = ccol[0]
        ccol[0] += n
        assert ccol[0] <= NCST, ccol[0]
        cmap[name] = c0
        return c0

    def load_cols(name, ap1d, n, p=128):
        c0 = newcols(name, n)
        dma('sp', V(cst.t[0:p, 0, c0:c0 + n], [('cst', name)]),
            V(ap1d.rearrange("(g p) -> p g", p=p), []), key=('cst', c0 % 8), slow=True)

    def cols(name, i0=0, n=1, p=128):
        c = cmap[name] + i0
        return V(cst.t[0:p, 0, c:c + n], [('cst', name)])

    def col(name, i=0, p=128):
        return cols(name, i, 1, p)

    for l in range(L):
        load_cols(('nmix', l), norm_mix_w[l], 8)
        load_cols(('nffn', l), norm_ffn_w[l], 8)
        load_cols(('hnw', l), hgrn_norm_w[l], 1)
        load_cols(('gnw', l), gla_norm_w[l], 1)
        for d in range(2):
            load_cols(('upb', l, d), gla_up_b[l, d], 4, p=64)
            load_cols(('ba', l, d), rglru_ba[l, d], 4)
            load_cols(('bx', l, d), rglru_bx[l, d], 4)
            load_cols(('lam', l, d), rglru_lam[l, d], 4)
        for j in range(4):
            load_cols(('ccw', l, j), c_conv_w[l, j], 4)
        load_cols(('ccb', l), c_conv_b[l], 4)
        for j in range(3):
            load_cols(('fcw', l, j), ffn_conv_w[l, j], 44)
        load_cols(('fcb', l), ffn_conv_b[l], 44)
    load_cols('fnw', final_norm_w, 8)
    for ll in range(4):
        for d in range(2):
            load_cols(('lbl', ll, d), lb_logits[ll, d], 4)
    for l in range(L):
        for d in range(2):
            for nm in ('upb', 'ba', 'bx'):
                newcols(('n' + nm, l, d), 4)
                p = 64 if nm == 'upb' else 128
                ts('dve', cols(('n' + nm, l, d), 0, 4, p), cols((nm, l, d), 0, 4, p), -1.0, None, ALU.mult)
            newcols(('clam', l, d), 4)
            newcols(('clam2', l, d), 4)
            c1 = cols(('clam', l, d), 0, 4)
            act_(c1, cols(('lam', l, d), 0, 4), AF.Exp, scale=-1.0)
            act_(c1, c1, AF.Ln, scale=1.0, bias=1.0)
            ts('dve', cols(('clam2', l, d), 0, 4), c1, -16.0, None, ALU.mult)
            ts('dve', c1, c1, -8.0, None, ALU.mult)
    for d in range(2):
        newcols(('lbe', d), 16)
        for ll in range(4):
            act_(cols(('lbe', d), 4 * ll, 4), cols(('lbl', ll, d), 0, 4), AF.Exp)
        newcols(('lbs', d), 4)
        sm = cols(('lbs', d), 0, 4)
        tt('dve', sm, cols(('lbe', d), 0, 4), cols(('lbe', d), 4, 4), ALU.add)
        tt('dve', sm, sm, cols(('lbe', d), 8, 4), ALU.add)
        tt('dve', sm, sm, cols(('lbe', d), 12, 4), ALU.add)
        rcp(sm, sm)
        for ll in range(4):
            newcols(('lb', ll, d), 4)
            newcols(('omlb', ll, d), 4)
            lbv = cols(('lb', ll, d), 0, 4)
            if ll == 0:
                memset('dve', lbv, 0.0)
            else:
                tt('dve', lbv, cols(('lbe', d), 4 * ll, 4), sm, ALU.mult)
                tt('dve', lbv, lbv, cols(('lb', ll - 1, d), 0, 4), ALU.add)
            ts('dve', cols(('omlb', ll, d), 0, 4), lbv, -1.0, 1.0, ALU.mult, ALU.add)

    for xs, nm in ((xA, 'xA'), (xB, 'xB')):
        for g in range(8):
            dma('sp', V(xs[g * 128:(g + 1) * 128, T + 2:XW], [nm]), V(tmp[10].t[:, 0, 0:510], tmp[10].all().keys), key=('zp', g))
            dma('sp', V(xs[g * 128:(g + 1) * 128, 0:2], [nm]), V(tmp[10].t[:, 0, 0:2], tmp[10].all().keys), key=('zp', g))

    cv_i = [0]
    ceng = ['act', 'dve']

    def conv_w(src2d, dst2d, R, C):
        cw = 2056 if C == NIN else min(C, 1408)
        assert C % cw == 0
        for r in range(R // 128):
            for c in range(C // cw):
                i = cv_i[0]
                cv_i[0] += 1
                s = i % 2
                fv = V(xt.t[:, 4 * s:4 * s + 4, :].rearrange("p g c -> p (g c)")[:, 0:cw], [('xt', 4 * s + q) for q in range(4)])
                bv = V(sqb.t[:, 4 * s:4 * s + 4, :].rearrange("p g c -> p (g c)")[:, 0:cw], [('sqb', 4 * s + q) for q in range(4)])
                dma('sp', fv, V(src2d[r * 128:(r + 1) * 128, c * cw:(c + 1) * cw], []), key=('cvf', s))
                cp(ceng[i % 2], bv, fv)
                dma('q_pool', V(dst2d[r * 128:(r + 1) * 128, c * cw:(c + 1) * cw], ['wbf']), bv, key=('cvb', s))
    for l in range(L):
        conv_w(w_in[l], wb_in[l], D, NIN)
        conv_w(w_branch[l].rearrange("b k d -> (b k) d"), wb_br[l], 1536, D)
        conv_w(w_out[l], wb_out[l], D, D)
        conv_w(ffn_up[l], wb_up[l], D, 2 * DFF)
        conv_w(ffn_down[l], wb_dn[l], DFF, D)

    def tokv(s, c0=None, c1=None):
        b = obuf[s % 2]
        return V(b.t[:, 0:2, :].rearrange("p g c -> p (g c)")[:, c0:c1], [(b.name, 0), (b.name, 1)])
    for n in range(NT):
        for s in range(4):
            dma('sp', tokv(s), V(x_in[n * TT + s * 128:n * TT + (s + 1) * 128, :], []), key=('tok', s % 2))
            for half in range(2):
                pb = 4 + half
                for gg in range(4):
                    g = half * 4 + gg
                    tr(P(pb, gg * 128, gg * 128 + 128), tokv(s, g * 128, g * 128 + 128), identf.all())
                outv = V(xt.t[:, half * 4:half * 4 + 4, 2 + s * 128:2 + s * 128 + 128], [('xt', half * 4 + gg) for gg in range(4)])
                inv = V(ps[pb][:, :].rearrange("p (g c) -> p g c", g=4), [('ps', pb)])
                cp('act' if half == 0 else 'dve', outv, inv)
        dma('q_pool', V(xA[:, n * TT + 2:n * TT + 2 + TT].rearrange("(g p) c -> p g c", p=128), ['xA']),
            V(xt.t[:, :, 2:514], xt.all().keys), key='stx')

    wrot = [0]

    def load_w(src2d, c0, w, kg=8):
        s = wrot[0] % NWS
        wrot[0] += 1
        sl = wsl[s]
        dma('sp', V(sl.t[:, 0:kg, 0:w], sl.all().keys),
            V(src2d[:, c0:c0 + w].rearrange("(k p) c -> p k c", p=128), ['wbf']), key=('w', s))
        return sl

    def rmsnorm(wname, c0, c1, outf=None):
        act_(V(sqb.t[:, :, c0:c1], sqb.all().keys), V(xt.t[:, :, c0:c1], xt.all().keys), AF.Square)
        for (a, b) in ((c0, min(c1, c0 + 512)), (c0 + 512, c1)):
            if b <= a:
                continue
            for g in range(8):
                mm(P(4, 0, b - a), onesb.all(), sqb.g(g, a, b), g == 0, g == 7)
            r = rstd.g(0, a, b)
            act_(r, P(4, 0, b - a), AF.Ln, scale=1.0 / D, bias=EPS)
            act_(r, r, AF.Exp, scale=-0.5)
        for g in range(8):
            o = hb.g(g, c0, c1) if outf is None else outf(g)
            stt('dve', o, xt.g(g, c0, c1), col(wname, g), rstd.g(0, c0, c1), ALU.mult, ALU.mult)

    def proj(sl, off, M, bank, rhs_c0=2, N=512, kg=8, rhs=None):
        src = hb if rhs is None else rhs
        for k in range(kg):
            mm(P(bank, 0, N, p=M), V(sl.t[:, k, off:off + M], sl.all().keys), src.g(k, rhs_c0, rhs_c0 + N), k == 0, k == kg - 1)

    def sigm_from(out, in_, t, scale=1.0, nbias=0.0):
        if isinstance(nbias, V):
            act_(t, in_, AF.Exp, scale=-scale, bias=nbias)
        else:
            act_(t, in_, AF.Relu, scale=scale, bias=40.0 - nbias)
            act_(t, t, AF.Exp, scale=-1.0, bias=40.0)
        act_(t, t, AF.Ln, scale=1.0, bias=1.0)
        act_(out, t, AF.Exp, scale=-1.0)

    def T_(i, c0=None, c1=None, p=128):
        return V(tmp[i].t[0:p, 0, c0:c1], [(tmp[i].name, 0)])

    def record(fn):
        saved = S.ops
        S.ops = []
        fn()
        out = S.ops
        S.ops = saved
        return out

    def interleave(lists):
        idx = [0] * len(lists)
        while True:
            live = [i for i in range(len(lists)) if idx[i] < len(lists[i])]
            if not live:
                break
            i = min(live, key=lambda q: idx[q] / len(lists[q]))
            S.ops.append(lists[i][idx[i]])
            idx[i] += 1

    def pipeline(n, stages):
        ns = len(stages)
        for t in range(n + ns - 1):
            for si in range(ns - 1, -1, -1):
                u = t - si
                if 0 <= u < n:
                    stages[si](u)

    subcnt = [0, 0]

    def gla_prep(h, dk, qv, kv, lgs, sc, rev, qscale, ib=8, ie=9, bufs=None, CL=32):
        p = dk
        qd, ki, ktl, lam = bufs[:4]
        nck = 512 // CL
        bs = T_(ib, 0, 512, p)
        E = T_(ie, 0, 512, p)
        if CL == 32:
            rm = rmB if rev else rmF
            if rev:
                scan(V(tmp[ib].t[0:p, 0, 511::-1], tmp[ib].all().keys), V(rm.t[0:p, 0, 511::-1], rm.all().keys),
                     V(lgs.ap[:, ::-1], lgs.keys), 0.0)
            else:
                scan(bs, V(rm.t[0:p, 0, :], rm.all().keys), lgs, 0.0)
        else:
            ones = V(onesb.t[0:p, 0, 0:CL], onesb.all().keys)
            for sg in range(nck):
                a, b = sg * CL, (sg + 1) * CL
                if rev:
                    scan(V(tmp[ib].t[0:p, 0, a:b][:, ::-1], tmp[ib].all().keys), ones, V(lgs.ap[:, a:b][:, ::-1], lgs.keys), 0.0)
                else:
                    scan(V(tmp[ib].t[0:p, 0, a:b], tmp[ib].all().keys), ones, V(lgs.ap[:, a:b], lgs.keys), 0.0)
        act_(E, bs, AF.Exp, scale=sc)
        stt('dve', qd.gp(h, p), qv, qscale, E, ALU.mult, ALU.mult)
        act_(E, bs, AF.Exp, scale=-sc)
        tt('pool', ki.gp(h, p), kv, E, ALU.mult)
        lc = 0 if rev else CL - 1
        bl = V(tmp[ib].t[0:p, 0, lc:512:CL], tmp[ib].all().keys)
        act_(V(lam.t[0:p, h, 0:nck], [(lam.name, h)]), bl, AF.Exp, scale=sc)
        blb = V(tmp[ib].t[0:p, 0, lc:512:CL].unsqueeze(2).to_broadcast([p, nck, CL]), tmp[ib].all().keys)
        E3 = V(tmp[ie].t[0:p, 0, 0:512].rearrange("p (c j) -> p c j", j=CL), tmp[ie].all().keys)
        bs3 = V(tmp[ib].t[0:p, 0, 0:512].rearrange("p (c j) -> p c j", j=CL), tmp[ib].all().keys)
        tt('dve', E3, blb, bs3, ALU.subtract)
        act_(E, E, AF.Exp, scale=sc)
        tt('pool', ktl.gp(h, p), kv, E, ALU.mult)

    def gla_core(mi, dk, rev, ob, accumulate, bufs, CL=32):
        p = dk
        qd, ki, ktl, lam, Vt = bufs
        nch = 128 // CL
        if CL == 32:
            mk = maskB if rev else maskF
        else:
            mk = maskB1 if rev else maskF1
        for s in (range(3, -1, -1) if rev else range(4)):
            si = subcnt[mi]
            subcnt[mi] += 1
            km = ktm[si % 2]
            am = attM[si % 2]
            i0 = nch * si
            c0 = s * 128
            for h in range(4):
                tr(V(psT[:, h * 128:h * 128 + dk], [('psT', 0)]), ktl.gp(h, p, c0, c0 + 128),
                   V(identb.t[0:p, 0, 0:p], identb.all().keys))
            if nch == 1:
                cp('dve', V(km.t[:, :, 0:dk], km.all().keys),
                   V(psT[:, 0:512].rearrange("p (h d) -> p h d", h=4)[:, :, 0:dk], [('psT', 0)]))
            else:
                for h in range(4):
                    in0 = V(psT[:, h * 128:h * 128 + dk].unsqueeze(1).to_broadcast([128, 4, dk]), [('psT', 0)])
                    in1 = V(cmask.t[:, 0, 0:4].unsqueeze(2).to_broadcast([128, 4, dk]), cmask.all().keys)
                    outv = V(km.t[:, h, :].rearrange("p (c d) -> p c d", c=4)[:, :, 0:dk], [(km.name, h)])
                    tt('dve', outv, in0, in1, ALU.mult)
            for h in range(4):
                mm(P(5, h * 128, h * 128 + 128), ki.gp(h, p, c0, c0 + 128), qd.gp(h, p, c0, c0 + 128), True, True)
            mkb = V(mk.t[:, 0, :].unsqueeze(1).to_broadcast([128, 4, 128]), mk.all().keys)
            tt('dve', V(am.t[:, 0, :].rearrange("p (h i) -> p h i", h=4), am.all().keys),
               V(ps[5][:, :].rearrange("p (h i) -> p h i", h=4), [('ps', 5)]), mkb, ALU.mult)
            corder = list(range(nch - 1, -1, -1) if rev else range(nch))
            KB = (6, 4, 6, 4)

            def kv_mm(ci):
                c = corder[ci]
                for h in range(4):
                    kmv = V(km.t[:, h, c * 128:c * 128 + dk], [(km.name, h)])
                    mm(P(KB[ci], h * 128, h * 128 + 128, p=dk), kmv, V(Vt.t[:, s, h * 128:(h + 1) * 128], [(Vt.name, s)]), True, True)
            for ci in range(min(2, nch)):
                kv_mm(ci)
            for ci, c in enumerate(corder):
                if ci in (1, 2) and ci + 1 < nch:
                    kv_mm(ci + 1)
                gch = s * nch + c
                for h in range(4):
                    sv = V(Scur[mi].t[0:p, h, :], [(Scur[mi].name, h)])
                    stt('dve', sv, sv, V(lam.t[0:p, h, gch:gch + 1], [(lam.name, h)]), P(KB[ci], h * 128, h * 128 + 128, p=dk), ALU.mult, ALU.add)
                slot = (i0 + ci + 1) % 5
                cp('act', V(Sbf[mi].t[0:p, :, slot * 128:(slot + 1) * 128], Sbf[mi].all().keys),
                   V(Scur[mi].t[0:p, :, :], Scur[mi].all().keys))
            mm(P(3), V(zerob.t[:, 0, 0:128], zerob.all().keys), zerob.all(), True, True)
            for h in range(4):
                mm(P(3, h * 128, h * 128 + 128), V(Vt.t[:, s, h * 128:(h + 1) * 128], [(Vt.name, s)]),
                   V(am.t[:, 0, h * 128:(h + 1) * 128], am.all().keys), False, True)
            for ci, c in enumerate(corder):
                slot = (i0 + ci) % 5
                for h in range(4):
                    mm(P(3, h * 128 + c * CL, h * 128 + c * CL + CL),
                       V(Sbf[mi].t[0:p, h, slot * 128:(slot + 1) * 128], [(Sbf[mi].name, h)]),
                       qd.gp(h, p, c0 + c * CL, c0 + c * CL + CL), False, True)
            outv = V(ob.t[:, :, c0:c0 + 128], ob.all().keys)
            inv = V(ps[3][:, :].rearrange("p (h i) -> p h i", h=4), [('ps', 3)])
            if accumulate:
                tt('dve', outv, inv, outv, ALU.add)
            else:
                cp('act', outv, inv)

    def v_proj(sl, vt):
        for s in range(4):
            for k in range(8):
                mm(P(0), hb.g(k, 2 + 128 * s, 2 + 128 * s + 128), sl.g(k), k == 0, k == 7)
            cp('act', V(vt.t[:, s, :], [(vt.name, s)]), P(0))

    def mixer_sweep(l, rev):
        d = 1 if rev else 0
        for mi in range(2):
            memset('pool', Scur[mi].all(), 0.0)
            memset('pool', Sbf[mi].all(), 0.0)
            subcnt[mi] = 0
        memset('pool', hst.all(), 0.0)
        order = list(range(NT - 1, -1, -1) if rev else range(NT))

        def ldx(nn):
            dma('q_pool', xt.all(), V(xA[:, nn * TT:nn * TT + 516].rearrange("(g p) c -> p g c", p=128), ['xA']), key='ldx')
        for oi, n in enumerate(order):
            if oi == 0:
                ldx(n)
            rmsnorm(('nmix', l), 0, 516)
            if oi + 1 < len(order):
                ldx(order[oi + 1])
            if not rev:
                for b in range(3):
                    dma('q_pool', obuf[b].all(), V(obwd[b * 512:(b + 1) * 512, n * TT:(n + 1) * TT].rearrange("(g p) c -> p g c", p=128), ['obwd']), key=('ldo', b))
            bufA = (qd, ki, ktl, lam, Vt)
            bufB = (qdB, kiB, ktlB, lamB, VtB)
            assert n_mix >= 3
            v_proj(load_w(wb_in[l], C_IA, 512), Vt)
            slq = load_w(wb_in[l], C_QA, 512)
            slz = load_w(wb_in[l], C_ZB if rev else C_ZF, 512)

            def a0(h):
                proj(slq, h * 128, 128, 2 * (h % 2))
                proj(slz, h * 128, 128, 2 * (h % 2) + 1)

            def a1(h):
                o = 5 * (h % 2)
                bq, bz = 2 * (h % 2), 2 * (h % 2) + 1
                sigm_from(T_(o, 0, 512), P(bq), T_(o, 0, 512))
                tt('dve', T_(o, 0, 512), T_(o, 0, 512), P(bq), ALU.mult)
                sigm_from(T_(o + 1, 0, 512), P(bz), T_(o + 1, 0, 512))
                ts('dve', T_(o + 1, 0, 512), T_(o + 1, 0, 512), col(('omlb', l, d), h), col(('lb', l, d), h), ALU.mult, ALU.add)
                act_(T_(o + 2, 0, 512), T_(o + 1, 0, 512), AF.Ln)
                ts('pool', T_(o + 1, 0, 512), T_(o + 1, 0, 512), -1.0, 1.0, ALU.mult, ALU.add)

            def a2(h):
                o = 5 * (h % 2)
                gla_prep(h, 128, T_(o, 0, 512), T_(o + 1, 0, 512), T_(o + 2, 0, 512), 1.0, rev, 128 ** -0.5, ib=o + 3, ie=o + 4, bufs=bufA)
            pipeline(4, [a0, a1, a2])

            def b_prep():
                v_proj(load_w(wb_in[l], C_VB, 512), VtB)
                slqk = load_w(wb_in[l], C_QB, 512)
                sllr = load_w(wb_in[l], C_LRF, 32)
                proj(sllr, 16 * d, 16, 0)
                cp('act', V(lrb.t[0:16, 0, :], lrb.all().keys), P(0, 0, 512, p=16))

                def b0(h):
                    proj(slqk, h * 64, 64, 0)
                    proj(slqk, 256 + h * 64, 64, 1)
                    mm(P(2, 0, 512, p=64), V(upw.t[0:16, d, h * 64:(h + 1) * 64], upw.all().keys), V(lrb.t[0:16, 0, :], lrb.all().keys), True, True)

                def b1(h):
                    o = 5 * (h % 2)
                    act_(T_(o + 2, 0, 512, 64), P(2, 0, 512, p=64), AF.Exp, scale=-1.0, bias=col(('nupb', l, d), h, p=64))
                    act_(T_(o + 2, 0, 512, 64), T_(o + 2, 0, 512, 64), AF.Ln, scale=1.0, bias=1.0)
                    cp('act', T_(o, 0, 512, 64), P(0, 0, 512, p=64))
                    cp('dve', T_(o + 1, 0, 512, 64), P(1, 0, 512, p=64))

                def b2(h):
                    o = 5 * (h % 2)
                    gla_prep(h, 64, T_(o, 0, 512, 64), T_(o + 1, 0, 512, 64), T_(o + 2, 0, 512, 64), -1.0 / 16.0, rev, 64 ** -0.5, ib=o + 3, ie=o + 4, bufs=bufB, CL=BCL)
                pipeline(4, [b0, b1, b2])
            a_core_ops = record(lambda: gla_core(0, 128, rev, obuf[0], not rev, bufA))
            b_prep_ops = record(b_prep)
            if INTERLEAVE:
                interleave([a_core_ops, b_prep_ops])
            else:
                S.ops.extend(a_core_ops)
                S.ops.extend(b_prep_ops)

            def c_mixer():
                slc = load_w(wb_in[l], C_XC, 512)
                for g in range(4):
                    proj(slc, g * 128, 128, 0, rhs_c0=0, N=512)
                    cp('act', T_(7, 0, 512), P(0))
                    proj(slc, g * 128, 128, 1, rhs_c0=512, N=4)
                    cp('dve', T_(7, 512, 516), P(1, 0, 4))
                    xc = T_(6, 0, 512)
                    act_(xc, T_(7, 0, 512), AF.Identity, scale=col(('ccw', l, 0), g), bias=col(('ccb', l), g))
                    for j in range(1, 4):
                        stt('dve', xc, T_(7, j, j + 512), col(('ccw', l, j), g), xc, ALU.mult, ALU.add)
                    cp('act', xcb.g(g), xc)
                    mm(P(0), wbd.g(d * 8 + g), xcb.g(g), True, True)
                    mm(P(1), wbd.g(d * 8 + 4 + g), xcb.g(g), True, True)
                    sigm_from(T_(0, 0, 512), P(0), T_(1, 0, 512), nbias=col(('nba', l, d), g))
                    sigm_from(T_(2, 0, 512), P(1), T_(1, 0, 512), nbias=col(('nbx', l, d), g))
                    act_(T_(4, 0, 512), T_(0, 0, 512), AF.Exp, scale=col(('clam', l, d), g))
                    act_(T_(5, 0, 512), T_(0, 0, 512), AF.Exp, scale=col(('clam2', l, d), g))
                    ts('pool', T_(5, 0, 512), T_(5, 0, 512), -1.0, 1.0, ALU.mult, ALU.add)
                    act_(T_(5, 0, 512), T_(5, 0, 512), AF.Ln)
                    act_(T_(5, 0, 512), T_(5, 0, 512), AF.Exp, scale=0.5)
                    tt('pool', T_(2, 0, 512), T_(2, 0, 512), xc, ALU.mult)
                    tt('dve', T_(2, 0, 512), T_(2, 0, 512), T_(5, 0, 512), ALU.mult)
                    hs = V(hst.t[:, g, d:d + 1], [('hst', g)])
                    if rev:
                        scan(V(tmp[1].t[:, 0, 511::-1], tmp[1].all().keys),
                             V(tmp[4].t[:, 0, 511::-1], tmp[4].all().keys), V(tmp[2].t[:, 0, 511::-1], tmp[2].all().keys), hs)
                        cp('dve', hs, T_(1, 0, 1))
                        cp('act', obuf[2].g(g), T_(1, 0, 512))
                    else:
                        scan(T_(1, 0, 512), T_(4, 0, 512), T_(2, 0, 512), hs)
                        cp('dve', hs, T_(1, 511, 512))
                        tt('pool', obuf[2].g(g), T_(1, 0, 512), obuf[2].g(g), ALU.add)


            b_core_ops = record(lambda: gla_core(1, 64, rev, obuf[1], not rev, bufB, CL=BCL))
            c_ops = record(c_mixer)
            if INTERLEAVE:
                interleave([b_core_ops, c_ops])
            else:
                S.ops.extend(b_core_ops)
                S.ops.extend(c_ops)
            if rev:
                for b in range(3):
                    dma('q_pool', V(obwd[b * 512:(b + 1) * 512, n * TT:(n + 1) * TT].rearrange("(g p) c -> p g c", p=128), ['obwd']),
                        obuf[b].all(), key=('sto', b))
                continue
            BK3 = ((0, 1), (2, 3), (5, 6))
            for b in range(2):
                if n_mix < b + 1:
                    memset('pool', obf[b].all(), 0.0)
                else:
                    act_(V(sqb.t[:, 4 * b:4 * b + 4, 0:512], [('sqb', 4 * b + q) for q in range(4)]), obuf[b].all(), AF.Square)
            slog = {}
            pcfg = ((C_OGA, 'hnw'), (C_OGB, 'gnw'))

            def pa0(u):
                b, h = u // 4, u % 4
                mm(P(BK3[u % 3][0]), onesb.all(), sqb.g(4 * b + h, 0, 512), True, True)

            def pa1(u):
                ta = T_(8 + u % 3, 0, 512)
                act_(ta, P(BK3[u % 3][0]), AF.Ln, scale=1.0 / 128.0, bias=EPS)
                act_(ta, ta, AF.Exp, scale=-0.5)

            def pa2(u):
                b, h = u // 4, u % 4
                stt('dve', T_(u, 0, 512), obuf[b].g(h), col((pcfg[b][1], l)), T_(8 + u % 3, 0, 512), ALU.mult, ALU.mult)
            pipeline(8, [pa0, pa1, pa2])

            def pb0(u):
                b, h = u // 4, u % 4
                if h == 0:
                    slog[b] = load_w(wb_in[l], pcfg[b][0], 512)
                proj(slog[b], h * 128, 128, BK3[u % 3][1])

            def pb1(u):
                act_(T_(8 + u % 3, 0, 512), P(BK3[u % 3][1]), AF.Silu)

            def pb2(u):
                b, h = u // 4, u % 4
                tt('dve', obf[b].g(h), T_(u, 0, 512), T_(8 + u % 3, 0, 512), ALU.mult)
            pipeline(8, [pb0, pb1, pb2])
            if n_mix >= 3:
                sly = load_w(wb_in[l], C_YC, 512)

                def y0(g):
                    proj(sly, g * 128, 128, g % 2)

                def y1(g):
                    act_(T_(g % 3, 0, 512), P(g % 2), AF.Gelu_apprx_tanh)

                def y2(g):
                    tt('dve', obf[2].g(g), T_(g % 3, 0, 512), obuf[2].g(g), ALU.mult)
                pipeline(4, [y0, y1, y2])
            else:
                memset('pool', obf[2].all(), 0.0)
            BK = ((0, 1), (2, 3), (5, 6))
            wl = {}

            def mg0(u):
                gc, br, gg = u // 12, (u % 12) // 4, u % 4
                if gg == 0:
                    wl[(gc, br)] = (load_w(wb_in[l], C_GA + br * 1024 + gc * 512, 512),
                                    load_w(wb_br[l, br * 512:(br + 1) * 512, :], gc * 512, 512, kg=4))
                slg, slb = wl[(gc, br)]
                proj(slg, gg * 128, 128, BK[u % 3][0])
                proj(slb, gg * 128, 128, BK[u % 3][1], rhs_c0=0, kg=4, rhs=obf[br])

            def mg1(u):
                act_(T_(u % 3, 0, 512), P(BK[u % 3][0]), AF.Sigmoid)

            def mg2(u):
                gc, br, gg = u // 12, (u % 12) // 4, u % 4
                g = gc * 4 + gg
                macc = T_(3 + g, 0, 512)
                sg = T_(u % 3, 0, 512)
                if br == 0:
                    tt('dve', macc, sg, P(BK[u % 3][1]), ALU.mult)
                else:
                    tt('dve', sg, sg, P(BK[u % 3][1]), ALU.mult)
                    tt('pool', macc, macc, sg, ALU.add)
                if br == 2:
                    cp('act', mbf.g(g), macc)
            pipeline(24, [mg0, mg1, mg2])
            wo = {}

            def wo0(u):
                if u % 4 == 0:
                    wo[u // 4] = load_w(wb_out[l], (u // 4) * 512, 512)
                dma('q_pool', T_(3 + u, 0, 512), V(xA[u * 128:(u + 1) * 128, n * TT + 2:n * TT + 2 + TT], ['xA']), key=('ldr', u % 4))
                proj(wo[u // 4], (u % 4) * 128, 128, u % 2, rhs_c0=0, rhs=mbf)

            def wo1(u):
                tt('dve', T_(3 + u, 0, 512), P(u % 2), T_(3 + u, 0, 512), ALU.add)
                dma('q_pool', V(xB[u * 128:(u + 1) * 128, n * TT + 2:n * TT + 2 + TT], ['xB']), T_(3 + u, 0, 512), key=('str', u % 4))
            pipeline(8, [wo0, wo1])

    def gelu_mul(outv, cg, cv, t1):
        act_(t1, cg, AF.Square)
        ts('dve', t1, t1, 0.044715, 1.0, ALU.mult, ALU.add)
        tt('dve', t1, t1, cg, ALU.mult)
        act_(t1, t1, AF.Relu, scale=1.5957691216057308, bias=40.0)
        act_(t1, t1, AF.Exp, scale=-1.0, bias=40.0)
        act_(t1, t1, AF.Ln, scale=1.0, bias=1.0)
        act_(t1, t1, AF.Exp, scale=-1.0)
        tt('pool', t1, t1, cg, ALU.mult)
        tt('dve', outv, t1, cv, ALU.mult)

    def FT(k, c0=None, c1=None):
        if k < 10:
            return T_(k, c0, c1)
        return obuf[(k - 10) // 4].g((k - 10) % 4, c0, c1)

    def ffn_sweep(l):
        def wlen(ww):
            return min(510, T - 510 * ww) + 2

        def ldxw(ww):
            W_ = wlen(ww)
            dma('q_pool', V(xt.t[:, :, 0:W_], xt.all().keys), V(xB[:, 510 * ww + 1:510 * ww + 1 + W_].rearrange("(g p) c -> p g c", p=128), ['xB']), key='ldx')
        for w in range(NWIN):
            c0 = 510 * w + 1
            nout = min(510, T - 510 * w)
            WL = nout + 2
            if w == 0:
                ldxw(0)
            rmsnorm(('nffn', l), 0, WL)
            if w + 1 < NWIN:
                ldxw(w + 1)
            wl = {}
            FB = ((0, 1), (2, 3), (5, 6))

            def f0(j):
                jb, gg = j // 4, j % 4
                if gg == 0:
                    ng = 4 if jb < 5 else 2
                    wl[jb] = (load_w(wb_up[l], jb * 512, ng * 128), load_w(wb_up[l], DFF + jb * 512, ng * 128))
                proj(wl[jb][0], gg * 128, 128, FB[j % 3][0], rhs_c0=0, N=WL)
                proj(wl[jb][1], gg * 128, 128, FB[j % 3][1], rhs_c0=0, N=WL)

            def f1(j):
                for q, fi in ((0, j), (1, 22 + j)):
                    bk = FB[j % 3][q]
                    C = FT(4 + 2 * (j % 4) + q, 1, WL - 1)
                    act_(C, P(bk, 0, WL - 2), AF.Identity, scale=col(('fcw', l, 0), fi), bias=col(('fcb', l), fi))
                    stt('dve', C, P(bk, 1, WL - 1), col(('fcw', l, 1), fi), C, ALU.mult, ALU.add)
                    stt('dve', C, P(bk, 2, WL), col(('fcw', l, 2), fi), C, ALU.mult, ALU.add)

            def f2(j):
                cg = FT(4 + 2 * (j % 4), 1, WL - 1)
                t1 = FT(12 + j % 3, 1, WL - 1)
                act_(t1, cg, AF.Gelu_apprx_tanh)

            def f3(j):
                cv = FT(4 + 2 * (j % 4) + 1, 1, WL - 1)
                t1 = FT(12 + j % 3, 1, WL - 1)
                tt('dve', actg(j, 0, nout), t1, cv, ALU.mult)
            pipeline(22, [f0, f1, f2, f3])
            for oc in range(2):
                for kb in range(3):
                    nk = 8 if kb < 2 else 6
                    sd = load_w(wb_dn[l, kb * 1024:kb * 1024 + nk * 128, :], oc * 512, 512, kg=nk)
                    for gg in range(4):
                        for kk in range(nk):
                            j = kb * 8 + kk
                            mm(P(gg, 0, nout), V(sd.t[:, kk, gg * 128:(gg + 1) * 128], sd.all().keys), actg(j, 0, nout), j == 0, j == 21)
                for gg in range(4):
                    g = oc * 4 + gg
                    r = lambda a, b: FT(15 + gg, a, b)
                    dma('q_pool', r(0, nout), V(xB[g * 128:(g + 1) * 128, c0 + 1:c0 + 1 + nout], ['xB']), key=('ldr', gg))
                    tt('dve', r(0, nout), P(gg, 0, nout), r(0, nout), ALU.add)
                    dma('q_pool', V(xA[g * 128:(g + 1) * 128, 510 * w + 2:510 * w + 2 + nout], ['xA']), r(0, nout), key=('str', gg))

    for l in range(L):
        dma('sp', upwf.all(), V(gla_up_w[l].rearrange("d r c -> r d c"), []), key='upw')
        cp('dve', upw.all(), upwf.all())
        for d in range(2):
            for ax, wsrc in ((0, rglru_wa), (1, rglru_wx)):
                stg = V(tmp[10].t[:, 0, 0:512].rearrange("p (g c) -> p g c", g=4), tmp[10].all().keys)
                memset('pool', stg, 0.0)
                for g in range(4):
                    for hh in range(2):
                        dma('sp', V(tmp[10].t[hh * 64:(hh + 1) * 64, 0, g * 128 + hh * 64:g * 128 + hh * 64 + 64], tmp[10].all().keys),
                            V(wsrc[l, d, 2 * g + hh], []), key=('wbd', 2 * g + hh))
                cp('dve', V(wbd.t[:, d * 8 + ax * 4:d * 8 + ax * 4 + 4, :], [('wbd', d * 8 + ax * 4 + q) for q in range(4)]), stg)
        mixer_sweep(l, True)
        mixer_sweep(l, False)
        ffn_sweep(l)

    for n in range(NT):
        dma('q_pool', V(xt.t[:, :, 0:512], xt.all().keys), V(xA[:, n * TT + 2:n * TT + 2 + TT].rearrange("(g p) c -> p g c", p=128), ['xA']), key='ldx')
        rmsnorm('fnw', 0, 512, outf=lambda g: obuf[1 + g // 4].g(g % 4))
        for s in range(4):
            for half in range(2):
                pb = 5 + half
                for gg in range(4):
                    g = half * 4 + gg
                    tr(P(pb, gg * 128, gg * 128 + 128), obuf[1 + g // 4].g(g % 4, s * 128, s * 128 + 128), identf.all())
                cp('act' if half == 0 else 'dve', V(obuf[0].t[:, 2 * (s % 2) + half, :], [('obuf0', 2 * (s % 2) + half)]), P(pb))
            dma('sp', V(y_out[n * TT + s * 128:n * TT + (s + 1) * 128, :], ['y']),
                V(obuf[0].t[:, 2 * (s % 2):2 * (s % 2) + 2, :].rearrange("p g c -> p (g c)"), [('obuf0', 2 * (s % 2)), ('obuf0', 2 * (s % 2) + 1)]), key=('sty', s % 2))
    st = S.emit()
    return nc, st


_CACHE = {}
WNAMES = ["norm_mix_w", "w_in", "hgrn_lb_logits", "hgrn_norm_w", "gla_up_w", "gla_up_b", "gla_norm_w", "c_conv_w",
          "c_conv_b", "rglru_wa", "rglru_ba", "rglru_wx", "rglru_bx", "rglru_lam", "w_branch", "w_out", "norm_ffn_w",
          "ffn_up", "ffn_conv_w", "ffn_conv_b", "ffn_down", "final_norm_w"]


def kernel(x_prompt, x_sample, **w):
    x_prompt = np.asarray(x_prompt, np.float32)
    x_sample = np.asarray(x_sample, np.float32)
    T = x_prompt.shape[1]
    L = np.asarray(w["w_in"]).shape[0]
    seqs = [x_prompt[i] for i in range(x_prompt.shape[0])] + [x_sample[i] for i in range(x_sample.shape[0])]
    key = (T, L)
    if key not in _CACHE:
        _CACHE[key] = build(T, L)[0]
    nc = _CACHE[key]
    wd = {k: np.ascontiguousarray(np.asarray(w[k], np.float32)) for k in WNAMES}
    in_maps = []
    for c in range(8):
        m = dict(wd)
        m["x"] = np.ascontiguousarray(seqs[c]) if c < len(seqs) else np.zeros((T, D), np.float32)
        in_maps.append(m)
    res = run_bass_kernel_spmd(nc, in_maps, core_ids=list(range(8)))
    ys = [np.asarray(res.results[c]["y"], np.float32) for c in range(len(seqs))]
    nb = x_prompt.shape[0]
    return (np.stack(ys[:nb], 0), np.stack(ys[nb:], 0))
```

```python
import numpy as np
import concourse.bass as bass
import concourse.mybir as mybir
from concourse.bass_utils import run_bass_kernel_spmd

F32 = mybir.dt.float32
BF16 = mybir.dt.bfloat16
AF = mybir.ActivationFunctionType
ALU = mybir.AluOpType

D = 1024
NIN = 8224
DFF = 2816
TT = 512
import os
INTERLEAVE = not os.environ.get('NOINT')
BCL = 128
EPS = 1e-6
C_QA, C_ZF, C_ZB, C_IA, C_OGA = 0, 512, 1024, 1536, 2048
C_QB, C_KB, C_VB, C_OGB, C_LRF, C_LRB = 2560, 2816, 3072, 3584, 4096, 4112
C_XC, C_YC, C_GA = 4128, 4640, 5152


class Sched:
    def __init__(self, nc):
        self.nc = nc
        self.ops = []
        self.engs = {'pe': nc.tensor, 'act': nc.scalar, 'dve': nc.vector, 'pool': nc.gpsimd, 'sp': nc.sync}

    def op(self, eng, fn, reads=(), writes=()):
        self.ops.append((eng, fn, tuple(reads), tuple(writes), None))

    def dma(self, q, fn, reads=(), writes=(), key=None):
        self.ops.append((q, fn, tuple(reads), tuple(writes), key))

    def emit(self):
        nc = self.nc
        ops = self.ops
        n = len(ops)
        writers, readers = {}, {}
        deps = [None] * n
        signal = [False] * n
        for i, (eng, fn, reads, writes, key) in enumerate(ops):
            stream = ('dma', key) if key is not None else eng
            d = set()
            for r in reads:
                w = writers.get(r)
                if w:
                    d.update(w.values())
            for wr in writes:
                w = writers.get(wr)
                if w:
                    d.update(w.values())
                rd = readers.get(wr)
                if rd:
                    d.update(rd.values())
            dl = []
            for j in d:
                if ops[j][4] is None and ops[j][0] == 'pe' and eng == 'pe' and key is None:
                    continue
                dl.append(j)
                signal[j] = True
            deps[i] = dl
            for r in reads:
                readers.setdefault(r, {})[stream] = i
            for wr in writes:
                writers.setdefault(wr, {})[stream] = i
        sems = {}

        def sem_for(name):
            if name not in sems:
                sems[name] = nc.alloc_semaphore('s_%d' % len(sems))
            return sems[name]
        count = {}
        sigval = [None] * n
        seen = {}
        nmap = {'q_act': 'act', 'q_pool': 'pool'}
        for i, (eng, fn, reads, writes, key) in enumerate(ops):
            ename = nmap.get(eng, eng)
            e = self.engs[ename]
            sn = seen.setdefault(ename, {})
            need = {}
            for j in deps[i]:
                s, v = sigval[j]
                if need.get(s, 0) < v:
                    need[s] = v
            for s, v in need.items():
                if sn.get(s, 0) >= v:
                    continue
                e.wait_ge(sem_for(s), v)
                sn[s] = v
            if key is not None:
                s = ('dma', key)
                pv = count.get(s, 0)
                if pv and sn.get(s, 0) < pv:
                    e.wait_ge(sem_for(s), pv)
                    sn[s] = pv
            inst = fn(e)
            if key is not None:
                s = ('dma', key)
                count[s] = count.get(s, 0) + 16
                inst.then_inc(sem_for(s), 16)
                sigval[i] = (s, count[s])
            elif signal[i]:
                s = ename
                count[s] = count.get(s, 0) + 1
                inst.then_inc(sem_for(s), 1)
                sigval[i] = (s, count[s])
        sp = self.engs['sp']
        for s, v in count.items():
            if isinstance(s, tuple):
                sp.wait_ge(sem_for(s), v)
        return dict(n_ops=n, n_sems=len(sems))


class V:
    def __init__(self, ap, keys):
        self.ap = ap
        self.keys = list(keys)


class Buf:
    def __init__(self, nc, name, shape, dtype):
        self.t = nc.alloc_sbuf_tensor(name, list(shape), dtype)
        self.name = name
        self.shape = shape

    def g(self, i, c0=None, c1=None):
        return V(self.t[:, i, c0:c1], [(self.name, i)])

    def gp(self, i, p, c0=None, c1=None):
        return V(self.t[0:p, i, c0:c1], [(self.name, i)])

    def all(self):
        n = self.shape[1] if len(self.shape) > 2 else 1
        return V(self.t[:], [(self.name, i) for i in range(n)])

    def v(self, ap_fn, idxs):
        return V(ap_fn(self.t), [(self.name, i) for i in idxs])


def build(T, L, n_mix=3):
    assert T % TT == 0
    NT = T // TT
    XW = T + 512
    NWIN = (T + 509) // 510
    nc = bass.Bass("TRN2", target_bir_lowering=False)
    S = Sched(nc)

    def din(name, shape):
        return nc.dram_tensor(name, list(shape), F32, kind="ExternalInput").ap()
    x_in = din("x", [T, D])
    norm_mix_w = din("norm_mix_w", [L, D])
    w_in = din("w_in", [L, D, NIN])
    lb_logits = din("hgrn_lb_logits", [4, 2, 512])
    hgrn_norm_w = din("hgrn_norm_w", [L, 128])
    gla_up_w = din("gla_up_w", [L, 2, 16, 256])
    gla_up_b = din("gla_up_b", [L, 2, 256])
    gla_norm_w = din("gla_norm_w", [L, 128])
    c_conv_w = din("c_conv_w", [L, 4, 512])
    c_conv_b = din("c_conv_b", [L, 512])
    rglru_wa = din("rglru_wa", [L, 2, 8, 64, 64])
    rglru_ba = din("rglru_ba", [L, 2, 512])
    rglru_wx = din("rglru_wx", [L, 2, 8, 64, 64])
    rglru_bx = din("rglru_bx", [L, 2, 512])
    rglru_lam = din("rglru_lam", [L, 2, 512])
    w_branch = din("w_branch", [L, 3, 512, D])
    w_out = din("w_out", [L, D, D])
    norm_ffn_w = din("norm_ffn_w", [L, D])
    ffn_up = din("ffn_up", [L, D, 2 * DFF])
    ffn_conv_w = din("ffn_conv_w", [L, 3, 2 * DFF])
    ffn_conv_b = din("ffn_conv_b", [L, 2 * DFF])
    ffn_down = din("ffn_down", [L, DFF, D])
    final_norm_w = din("final_norm_w", [D])
    y_out = nc.dram_tensor("y", [T, D], F32, kind="ExternalOutput").ap()

    def dscr(name, shape, dt):
        return nc.dram_tensor(name, list(shape), dt, kind="Internal").ap()
    xA = dscr("xA", [D, XW], F32)
    xB = dscr("xB", [D, XW], F32)
    obwd = dscr("obwd", [1536, T], F32)
    wb_in = dscr("wb_in", [L, D, NIN], BF16)
    wb_br = dscr("wb_br", [L, 1536, D], BF16)
    wb_out = dscr("wb_out", [L, D, D], BF16)
    wb_up = dscr("wb_up", [L, D, 2 * DFF], BF16)
    wb_dn = dscr("wb_dn", [L, DFF, D], BF16)

    ps = [nc.alloc_psum_tensor("ps%d" % i, [128, 512], F32) for i in range(7)]
    psT = nc.alloc_psum_tensor("psT", [128, 1024], BF16)

    def P(i, c0=None, c1=None, p=128):
        return V(ps[i][0:p, c0:c1], [('ps', i)])

    xt = Buf(nc, "xt", [128, 8, 516], F32)
    sqb = Buf(nc, "sqb", [128, 8, 516], BF16)
    hb = Buf(nc, "hb", [128, 8, 516], BF16)
    rstd = Buf(nc, "rstd", [128, 1, 516], F32)
    NWS = 3
    wsl = [Buf(nc, "wsl%d" % i, [128, 8, 512], BF16) for i in range(NWS)]
    tmp = [Buf(nc, "tmp%d" % i, [128, 1, 516], F32) for i in range(11)]
    qd = Buf(nc, "qd", [128, 4, 512], BF16)
    ki = Buf(nc, "ki", [128, 4, 512], BF16)
    ktl = Buf(nc, "ktl", [128, 4, 512], BF16)
    ktm = [Buf(nc, "ktm%d" % i, [128, 4, 512], BF16) for i in range(2)]
    Vt = Buf(nc, "Vt", [128, 4, 512], BF16)
    qdB = Buf(nc, "qdB", [128, 4, 512], BF16)
    kiB = Buf(nc, "kiB", [128, 4, 512], BF16)
    ktlB = Buf(nc, "ktlB", [128, 4, 512], BF16)
    VtB = Buf(nc, "VtB", [128, 4, 512], BF16)
    lamB = Buf(nc, "lamB", [128, 4, 16], F32)
    attM = [Buf(nc, "attM%d" % i, [128, 1, 512], BF16) for i in range(2)]
    lam = Buf(nc, "lam", [128, 4, 16], F32)
    Scur = [Buf(nc, "Scur%d" % i, [128, 4, 128], F32) for i in range(2)]
    Sbf = [Buf(nc, "Sbf%d" % i, [128, 4, 640], BF16) for i in range(2)]
    obuf = [Buf(nc, "obuf%d" % i, [128, 4, 512], F32) for i in range(3)]
    obf = [Buf(nc, "obf%d" % i, [128, 4, 512], BF16) for i in range(3)]
    xcb = Buf(nc, "xcb", [128, 4, 512], BF16)
    hst = Buf(nc, "hst", [128, 4, 2], F32)
    mbf = Buf(nc, "mbf", [128, 8, 512], BF16)
    wbd = Buf(nc, "wbd", [128, 16, 128], BF16)
    upw = Buf(nc, "upw", [16, 2, 256], BF16)
    upwf = Buf(nc, "upwf", [16, 2, 256], F32)
    lrb = Buf(nc, "lrb", [16, 1, 512], BF16)
    NCST = 1600
    cst = Buf(nc, "cst", [128, 1, NCST], F32)
    identf = Buf(nc, "identf", [128, 1, 128], F32)
    identb = Buf(nc, "identb", [128, 1, 128], BF16)
    onesb = Buf(nc, "onesb", [128, 1, 128], BF16)
    zerob = Buf(nc, "zerob", [128, 1, 512], BF16)
    maskF = Buf(nc, "maskF", [128, 1, 128], F32)
    maskB = Buf(nc, "maskB", [128, 1, 128], F32)
    cmask = Buf(nc, "cmask", [128, 1, 4], F32)
    maskF1 = Buf(nc, "maskF1", [128, 1, 128], F32)
    maskB1 = Buf(nc, "maskB1", [128, 1, 128], F32)
    rmF = Buf(nc, "rmF", [128, 1, 512], F32)
    rmB = Buf(nc, "rmB", [128, 1, 512], F32)

    def actg(j, c0=None, c1=None):
        if j < 12:
            return obf[j // 4].g(j % 4, c0, c1)
        if j < 20:
            return mbf.g(j - 12, c0, c1)
        return xcb.g(j - 20, c0, c1)

    def k_(*vs):
        r = []
        for v in vs:
            if isinstance(v, V):
                r += v.keys
        return r

    def act_(out, in_, func, scale=1.0, bias=0.0):
        sc = scale.ap if isinstance(scale, V) else scale
        bi = bias.ap if isinstance(bias, V) else bias
        S.op('act', lambda e: e.activation(out=out.ap, in_=in_.ap, func=func, bias=bi, scale=sc),
             reads=k_(in_, scale, bias), writes=out.keys)

    def tt(eng, out, a, b, op):
        S.op(eng, lambda e: e.tensor_tensor(out=out.ap, in0=a.ap, in1=b.ap, op=op), reads=k_(a, b), writes=out.keys)

    def ts(eng, out, a, s1, s2, op0, op1=None):
        s1a = s1.ap if isinstance(s1, V) else s1
        s2a = s2.ap if isinstance(s2, V) else s2
        if op1 is None:
            S.op(eng, lambda e: e.tensor_scalar(out=out.ap, in0=a.ap, scalar1=s1a, scalar2=None, op0=op0),
                 reads=k_(a, s1), writes=out.keys)
        else:
            S.op(eng, lambda e: e.tensor_scalar(out=out.ap, in0=a.ap, scalar1=s1a, scalar2=s2a, op0=op0, op1=op1),
                 reads=k_(a, s1, s2), writes=out.keys)

    def stt(eng, out, a, sc, b, op0, op1):
        sca = sc.ap if isinstance(sc, V) else sc
        S.op(eng, lambda e: e.scalar_tensor_tensor(out=out.ap, in0=a.ap, scalar=sca, in1=b.ap, op0=op0, op1=op1),
             reads=k_(a, sc, b), writes=out.keys)

    def cp(eng, out, a):
        if eng == 'act':
            act_(out, a, AF.Copy)
        else:
            S.op(eng, lambda e: e.tensor_copy(out=out.ap, in_=a.ap), reads=a.keys, writes=out.keys)

    def rcp(out, a):
        S.op('dve', lambda e: e.reciprocal(out=out.ap, in_=a.ap), reads=a.keys, writes=out.keys)

    def scan(out, d0, d1, init):
        ia = init.ap if isinstance(init, V) else init
        S.op('dve', lambda e: e.tensor_tensor_scan(out=out.ap, data0=d0.ap, data1=d1.ap, initial=ia,
                                                    op0=ALU.mult, op1=ALU.add),
             reads=k_(d0, d1, init), writes=out.keys)

    def mm(out, lhsT, rhs, start, stop):
        S.op('pe', lambda e: e.matmul(out.ap, lhsT=lhsT.ap, rhs=rhs.ap, start=start, stop=stop, skip_group_check=True),
             reads=k_(lhsT, rhs), writes=out.keys)

    def tr(out, in_, ident):
        S.op('pe', lambda e: e.transpose(out=out.ap, in_=in_.ap, identity=ident.ap), reads=k_(in_, ident), writes=out.keys)

    def memset(eng, out, val):
        S.op(eng, lambda e: e.memset(out.ap, val), writes=out.keys)

    def dma(q, out, in_, key, slow=False):
        S.dma(q, lambda e: e.dma_start(out=out.ap, in_=in_.ap, allow_slow_non_contiguous=slow),
              reads=in_.keys, writes=out.keys, key=key)

    def asel(out, pattern, cmp, base, cm):
        S.op('pool', lambda e: e.affine_select(out=out.ap, in_=out.ap, pattern=pattern, compare_op=cmp, fill=0.0,
                                               base=base, channel_multiplier=cm), reads=out.keys, writes=out.keys)

    memset('pool', identf.all(), 1.0)
    asel(identf.all(), [[-1, 128]], ALU.is_equal, 0, 1)
    cp('dve', identb.all(), identf.all())
    memset('pool', onesb.all(), 1.0)
    memset('pool', zerob.all(), 0.0)
    memset('pool', tmp[10].all(), 0.0)
    for mk, pat, cm in ((maskF, [[1, 128]], -1), (maskB, [[-1, 128]], 1)):
        memset('pool', mk.all(), 1.0)
        asel(mk.all(), pat, ALU.is_ge, 0, cm)
        for c in range(4):
            blk = mk.g(0, 32 * c, 32 * c + 32)
            asel(blk, [[0, 32]], ALU.is_ge, -32 * c, 1)
            asel(blk, [[0, 32]], ALU.is_ge, 32 * c + 31, -1)
    for mk, pat, cm in ((maskF1, [[1, 128]], -1), (maskB1, [[-1, 128]], 1)):
        memset('pool', mk.all(), 1.0)
        asel(mk.all(), pat, ALU.is_ge, 0, cm)
    memset('pool', cmask.all(), 1.0)
    for c in range(4):
        blk = cmask.g(0, c, c + 1)
        asel(blk, [[0, 1]], ALU.is_ge, -32 * c, 1)
        asel(blk, [[0, 1]], ALU.is_ge, 32 * c + 31, -1)
    memset('pool', rmF.all(), 1.0)
    memset('pool', rmB.all(), 1.0)
    memset('pool', rmF.v(lambda t: t[:, 0, 0:512:32], [0]), 0.0)
    memset('pool', rmB.v(lambda t: t[:, 0, 31:512:32], [0]), 0.0)

    ccol = [0]
    cmap = {}

    def newcols(name, n):
        c0 = ccol[0]
        ccol[0] += n
        assert ccol[0] <= NCST, ccol[0]
        cmap[name] = c0
        return c0

    def load_cols(name, ap1d, n, p=128):
        c0 = newcols(name, n)
        dma('sp', V(cst.t[0:p, 0, c0:c0 + n], [('cst', name)]),
            V(ap1d.rearrange("(g p) -> p g", p=p), []), key=('cst', c0 % 8), slow=True)

    def cols(name, i0=0, n=1, p=128):
        c = cmap[name] + i0
        return V(cst.t[0:p, 0, c:c + n], [('cst', name)])

    def col(name, i=0, p=128):
        return cols(name, i, 1, p)

    for l in range(L):
        load_cols(('nmix', l), norm_mix_w[l], 8)
        load_cols(('nffn', l), norm_ffn_w[l], 8)
        load_cols(('hnw', l), hgrn_norm_w[l], 1)
        load_cols(('gnw', l), gla_norm_w[l], 1)
        for d in range(2):
            load_cols(('upb', l, d), gla_up_b[l, d], 4, p=64)
            load_cols(('ba', l, d), rglru_ba[l, d], 4)
            load_cols(('bx', l, d), rglru_bx[l, d], 4)
            load_cols(('lam', l, d), rglru_lam[l, d], 4)
        for j in range(4):
            load_cols(('ccw', l, j), c_conv_w[l, j], 4)
        load_cols(('ccb', l), c_conv_b[l], 4)
        for j in range(3):
            load_cols(('fcw', l, j), ffn_conv_w[l, j], 44)
        load_cols(('fcb', l), ffn_conv_b[l], 44)
    load_cols('fnw', final_norm_w, 8)
    for ll in range(4):
        for d in range(2):
            load_cols(('lbl', ll, d), lb_logits[ll, d], 4)
    for l in range(L):
        for d in range(2):
            for nm in ('upb', 'ba', 'bx'):
                newcols(('n' + nm, l, d), 4)
                p = 64 if nm == 'upb' else 128
                ts('dve', cols(('n' + nm, l, d), 0, 4, p), cols((nm, l, d), 0, 4, p), -1.0, None, ALU.mult)
            newcols(('clam', l, d), 4)
            newcols(('clam2', l, d), 4)
            c1 = cols(('clam', l, d), 0, 4)
            act_(c1, cols(('lam', l, d), 0, 4), AF.Exp, scale=-1.0)
            act_(c1, c1, AF.Ln, scale=1.0, bias=1.0)
            ts('dve', cols(('clam2', l, d), 0, 4), c1, -16.0, None, ALU.mult)
            ts('dve', c1, c1, -8.0, None, ALU.mult)
    for d in range(2):
        newcols(('lbe', d), 16)
        for ll in range(4):
            act_(cols(('lbe', d), 4 * ll, 4), cols(('lbl', ll, d), 0, 4), AF.Exp)
        newcols(('lbs', d), 4)
        sm = cols(('lbs', d), 0, 4)
        tt('dve', sm, cols(('lbe', d), 0, 4), cols(('lbe', d), 4, 4), ALU.add)
        tt('dve', sm, sm, cols(('lbe', d), 8, 4), ALU.add)
        tt('dve', sm, sm, cols(('lbe', d), 12, 4), ALU.add)
        rcp(sm, sm)
        for ll in range(4):
            newcols(('lb', ll, d), 4)
            newcols(('omlb', ll, d), 4)
            lbv = cols(('lb', ll, d), 0, 4)
            if ll == 0:
                memset('dve', lbv, 0.0)
            else:
                tt('dve', lbv, cols(('lbe', d), 4 * ll, 4), sm, ALU.mult)
                tt('dve', lbv, lbv, cols(('lb', ll - 1, d), 0, 4), ALU.add)
            ts('dve', cols(('omlb', ll, d), 0, 4), lbv, -1.0, 1.0, ALU.mult, ALU.add)

    for xs, nm in ((xA, 'xA'), (xB, 'xB')):
        for g in range(8):
            dma('sp', V(xs[g * 128:(g + 1) * 128, T + 2:XW], [nm]), V(tmp[10].t[:, 0, 0:510], tmp[10].all().keys), key=('zp', g))
            dma('sp', V(xs[g * 128:(g + 1) * 128, 0:2], [nm]), V(tmp[10].t[:, 0, 0:2], tmp[10].all().keys), key=('zp', g))

    cv_i = [0]
    ceng = ['act', 'dve']

    def conv_w(src2d, dst2d, R, C):
        cw = 2056 if C == NIN else min(C, 1408)
        assert C % cw == 0
        for r in range(R // 128):
            for c in range(C // cw):
                i = cv_i[0]
                cv_i[0] += 1
                s = i % 2
                fv = V(xt.t[:, 4 * s:4 * s + 4, :].rearrange("p g c -> p (g c)")[:, 0:cw], [('xt', 4 * s + q) for q in range(4)])
                bv = V(sqb.t[:, 4 * s:4 * s + 4, :].rearrange("p g c -> p (g c)")[:, 0:cw], [('sqb', 4 * s + q) for q in range(4)])
                dma('sp', fv, V(src2d[r * 128:(r + 1) * 128, c * cw:(c + 1) * cw], []), key=('cvf', s))
                cp(ceng[i % 2], bv, fv)
                dma('q_pool', V(dst2d[r * 128:(r + 1) * 128, c * cw:(c + 1) * cw], ['wbf']), bv, key=('cvb', s))
    for l in range(L):
        conv_w(w_in[l], wb_in[l], D, NIN)
        conv_w(w_branch[l].rearrange("b k d -> (b k) d"), wb_br[l], 1536, D)
        conv_w(w_out[l], wb_out[l], D, D)
        conv_w(ffn_up[l], wb_up[l], D, 2 * DFF)
        conv_w(ffn_down[l], wb_dn[l], DFF, D)

    def tokv(s, c0=None, c1=None):
        b = obuf[s % 2]
        return V(b.t[:, 0:2, :].rearrange("p g c -> p (g c)")[:, c0:c1], [(b.name, 0), (b.name, 1)])
    for n in range(NT):
        for s in range(4):
            dma('sp', tokv(s), V(x_in[n * TT + s * 128:n * TT + (s + 1) * 128, :], []), key=('tok', s % 2))
            for half in range(2):
                pb = 4 + half
                for gg in range(4):
                    g = half * 4 + gg
                    tr(P(pb, gg * 128, gg * 128 + 128), tokv(s, g * 128, g * 128 + 128), identf.all())
                outv = V(xt.t[:, half * 4:half * 4 + 4, 2 + s * 128:2 + s * 128 + 128], [('xt', half * 4 + gg) for gg in range(4)])
                inv = V(ps[pb][:, :].rearrange("p (g c) -> p g c", g=4), [('ps', pb)])
                cp('act' if half == 0 else 'dve', outv, inv)
        dma('q_pool', V(xA[:, n * TT + 2:n * TT + 2 + TT].rearrange("(g p) c -> p g c", p=128), ['xA']),
            V(xt.t[:, :, 2:514], xt.all().keys), key='stx')

    wrot = [0]

    def load_w(src2d, c0, w, kg=8):
        s = wrot[0] % NWS
        wrot[0] += 1
        sl = wsl[s]
        dma('sp', V(sl.t[:, 0:kg, 0:w], sl.all().keys),
            V(src2d[:, c0:c0 + w].rearrange("(k p) c -> p k c", p=128), ['wbf']), key=('w', s))
        return sl

    def rmsnorm(wname, c0, c1, outf=None):
        act_(V(sqb.t[:, :, c0:c1], sqb.all().keys), V(xt.t[:, :, c0:c1], xt.all().keys), AF.Square)
        for (a, b) in ((c0, min(c1, c0 + 512)), (c0 + 512, c1)):
            if b <= a:
                continue
            for g in range(8):
                mm(P(4, 0, b - a), onesb.all(), sqb.g(g, a, b), g == 0, g == 7)
            r = rstd.g(0, a, b)
            act_(r, P(4, 0, b - a), AF.Ln, scale=1.0 / D, bias=EPS)
            act_(r, r, AF.Exp, scale=-0.5)
        for g in range(8):
            o = hb.g(g, c0, c1) if outf is None else outf(g)
            stt('dve', o, xt.g(g, c0, c1), col(wname, g), rstd.g(0, c0, c1), ALU.mult, ALU.mult)

    def proj(sl, off, M, bank, rhs_c0=2, N=512, kg=8, rhs=None):
        src = hb if rhs is None else rhs
        for k in range(kg):
            mm(P(bank, 0, N, p=M), V(sl.t[:, k, off:off + M], sl.all().keys), src.g(k, rhs_c0, rhs_c0 + N), k == 0, k == kg - 1)

    def sigm_from(out, in_, t, scale=1.0, nbias=0.0):
        if isinstance(nbias, V):
            act_(t, in_, AF.Exp, scale=-scale, bias=nbias)
        else:
            act_(t, in_, AF.Relu, scale=scale, bias=40.0 - nbias)
            act_(t, t, AF.Exp, scale=-1.0, bias=40.0)
        act_(t, t, AF.Ln, scale=1.0, bias=1.0)
        act_(out, t, AF.Exp, scale=-1.0)

    def T_(i, c0=None, c1=None, p=128):
        return V(tmp[i].t[0:p, 0, c0:c1], [(tmp[i].name, 0)])

    def record(fn):
        saved = S.ops
        S.ops = []
        fn()
        out = S.ops
        S.ops = saved
        return out

    def interleave(lists):
        idx = [0] * len(lists)
        while True:
            live = [i for i in range(len(lists)) if idx[i] < len(lists[i])]
            if not live:
                break
            i = min(live, key=lambda q: idx[q] / len(lists[q]))
            S.ops.append(lists[i][idx[i]])
            idx[i] += 1

    def pipeline(n, stages):
        ns = len(stages)
        for t in range(n + ns - 1):
            for si in range(ns - 1, -1, -1):
                u = t - si
                if 0 <= u < n:
                    stages[si](u)

    subcnt = [0, 0]

    def gla_prep(h, dk, qv, kv, lgs, sc, rev, qscale, ib=8, ie=9, bufs=None, CL=32):
        p = dk
        qd, ki, ktl, lam = bufs[:4]
        nck = 512 // CL
        bs = T_(ib, 0, 512, p)
        E = T_(ie, 0, 512, p)
        if CL == 32:
            rm = rmB if rev else rmF
            if rev:
                scan(V(tmp[ib].t[0:p, 0, 511::-1], tmp[ib].all().keys), V(rm.t[0:p, 0, 511::-1], rm.all().keys),
                     V(lgs.ap[:, ::-1], lgs.keys), 0.0)
            else:
                scan(bs, V(rm.t[0:p, 0, :], rm.all().keys), lgs, 0.0)
        else:
            ones = V(onesb.t[0:p, 0, 0:CL], onesb.all().keys)
            for sg in range(nck):
                a, b = sg * CL, (sg + 1) * CL
                if rev:
                    scan(V(tmp[ib].t[0:p, 0, a:b][:, ::-1], tmp[ib].all().keys), ones, V(lgs.ap[:, a:b][:, ::-1], lgs.keys), 0.0)
                else:
                    scan(V(tmp[ib].t[0:p, 0, a:b], tmp[ib].all().keys), ones, V(lgs.ap[:, a:b], lgs.keys), 0.0)
        act_(E, bs, AF.Exp, scale=sc)
        stt('dve', qd.gp(h, p), qv, qscale, E, ALU.mult, ALU.mult)
        act_(E, bs, AF.Exp, scale=-sc)
        tt('pool', ki.gp(h, p), kv, E, ALU.mult)
        lc = 0 if rev else CL - 1
        bl = V(tmp[ib].t[0:p, 0, lc:512:CL], tmp[ib].all().keys)
        act_(V(lam.t[0:p, h, 0:nck], [(lam.name, h)]), bl, AF.Exp, scale=sc)
        blb = V(tmp[ib].t[0:p, 0, lc:512:CL].unsqueeze(2).to_broadcast([p, nck, CL]), tmp[ib].all().keys)
        E3 = V(tmp[ie].t[0:p, 0, 0:512].rearrange("p (c j) -> p c j", j=CL), tmp[ie].all().keys)
        bs3 = V(tmp[ib].t[0:p, 0, 0:512].rearrange("p (c j) -> p c j", j=CL), tmp[ib].all().keys)
        tt('dve', E3, blb, bs3, ALU.subtract)
        act_(E, E, AF.Exp, scale=sc)
        tt('pool', ktl.gp(h, p), kv, E, ALU.mult)

    def gla_core(mi, dk, rev, ob, accumulate, bufs, CL=32):
        p = dk
        qd, ki, ktl, lam, Vt = bufs
        nch = 128 // CL
        if CL == 32:
            mk = maskB if rev else maskF
        else:
            mk = maskB1 if rev else maskF1
        for s in (range(3, -1, -1) if rev else range(4)):
            si = subcnt[mi]
            subcnt[mi] += 1
            km = ktm[si % 2]
            am = attM[si % 2]
            i0 = nch * si
            c0 = s * 128
            for h in range(4):
                tr(V(psT[:, h * 128:h * 128 + dk], [('psT', 0)]), ktl.gp(h, p, c0, c0 + 128),
                   V(identb.t[0:p, 0, 0:p], identb.all().keys))
            if nch == 1:
                cp('dve', V(km.t[:, :, 0:dk], km.all().keys),
                   V(psT[:, 0:512].rearrange("p (h d) -> p h d", h=4)[:, :, 0:dk], [('psT', 0)]))
            else:
                for h in range(4):
                    in0 = V(psT[:, h * 128:h * 128 + dk].unsqueeze(1).to_broadcast([128, 4, dk]), [('psT', 0)])
                    in1 = V(cmask.t[:, 0, 0:4].unsqueeze(2).to_broadcast([128, 4, dk]), cmask.all().keys)
                    outv = V(km.t[:, h, :].rearrange("p (c d) -> p c d", c=4)[:, :, 0:dk], [(km.name, h)])
                    tt('dve', outv, in0, in1, ALU.mult)
            for h in range(4):
                mm(P(5, h * 128, h * 128 + 128), ki.gp(h, p, c0, c0 + 128), qd.gp(h, p, c0, c0 + 128), True, True)
            mkb = V(mk.t[:, 0, :].unsqueeze(1).to_broadcast([128, 4, 128]), mk.all().keys)
            tt('dve', V(am.t[:, 0, :].rearrange("p (h i) -> p h i", h=4), am.all().keys),
               V(ps[5][:, :].rearrange("p (h i) -> p h i", h=4), [('ps', 5)]), mkb, ALU.mult)
            corder = list(range(nch - 1, -1, -1) if rev else range(nch))
            KB = (6, 4, 6, 4)

            def kv_mm(ci):
                c = corder[ci]
                for h in range(4):
                    kmv = V(km.t[:, h, c * 128:c * 128 + dk], [(km.name, h)])
                    mm(P(KB[ci], h * 128, h * 128 + 128, p=dk), kmv, V(Vt.t[:, s, h * 128:(h + 1) * 128], [(Vt.name, s)]), True, True)
            for ci in range(min(2, nch)):
                kv_mm(ci)
            for ci, c in enumerate(corder):
                if ci in (1, 2) and ci + 1 < nch:
                    kv_mm(ci + 1)
                gch = s * nch + c
                for h in range(4):
                    sv = V(Scur[mi].t[0:p, h, :], [(Scur[mi].name, h)])
                    stt('dve', sv, sv, V(lam.t[0:p, h, gch:gch + 1], [(lam.name, h)]), P(KB[ci], h * 128, h * 128 + 128, p=dk), ALU.mult, ALU.add)
                slot = (i0 + ci + 1) % 5
                cp('act', V(Sbf[mi].t[0:p, :, slot * 128:(slot + 1) * 128], Sbf[mi].all().keys),
                   V(Scur[mi].t[0:p, :, :], Scur[mi].all().keys))
            mm(P(3), V(zerob.t[:, 0, 0:128], zerob.all().keys), zerob.all(), True, True)
            for h in range(4):
                mm(P(3, h * 128, h * 128 + 128), V(Vt.t[:, s, h * 128:(h + 1) * 128], [(Vt.name, s)]),
                   V(am.t[:, 0, h * 128:(h + 1) * 128], am.all().keys), False, True)
            for ci, c in enumerate(corder):
                slot = (i0 + ci) % 5
                for h in range(4):
                    mm(P(3, h * 128 + c * CL, h * 128 + c * CL + CL),
                       V(Sbf[mi].t[0:p, h, slot * 128:(slot + 1) * 128], [(Sbf[mi].name, h)]),
                       qd.gp(h, p, c0 + c * CL, c0 + c * CL + CL), False, True)
            outv = V(ob.t[:, :, c0:c0 + 128], ob.all().keys)
            inv = V(ps[3][:, :].rearrange("p (h i) -> p h i", h=4), [('ps', 3)])
            if accumulate:
                tt('dve', outv, inv, outv, ALU.add)
            else:
                cp('act', outv, inv)

    def v_proj(sl, vt):
        for s in range(4):
            for k in range(8):
                mm(P(0), hb.g(k, 2 + 128 * s, 2 + 128 * s + 128), sl.g(k), k == 0, k == 7)
            cp('act', V(vt.t[:, s, :], [(vt.name, s)]), P(0))

    def mixer_sweep(l, rev):
        d = 1 if rev else 0
        for mi in range(2):
            memset('pool', Scur[mi].all(), 0.0)
            memset('pool', Sbf[mi].all(), 0.0)
            subcnt[mi] = 0
        memset('pool', hst.all(), 0.0)
        order = list(range(NT - 1, -1, -1) if rev else range(NT))

        def ldx(nn):
            dma('q_pool', xt.all(), V(xA[:, nn * TT:nn * TT + 516].rearrange("(g p) c -> p g c", p=128), ['xA']), key='ldx')
        for oi, n in enumerate(order):
            if oi == 0:
                ldx(n)
            rmsnorm(('nmix', l), 0, 516)
            if oi + 1 < len(order):
                ldx(order[oi + 1])
            if not rev:
                for b in range(3):
                    dma('q_pool', obuf[b].all(), V(obwd[b * 512:(b + 1) * 512, n * TT:(n + 1) * TT].rearrange("(g p) c -> p g c", p=128), ['obwd']), key=('ldo', b))
            bufA = (qd, ki, ktl, lam, Vt)
            bufB = (qdB, kiB, ktlB, lamB, VtB)
            assert n_mix >= 3
            v_proj(load_w(wb_in[l], C_IA, 512), Vt)
            slq = load_w(wb_in[l], C_QA, 512)
            slz = load_w(wb_in[l], C_ZB if rev else C_ZF, 512)

            def a0(h):
                proj(slq, h * 128, 128, 2 * (h % 2))
                proj(slz, h * 128, 128, 2 * (h % 2) + 1)

            def a1(h):
                o = 5 * (h % 2)
                bq, bz = 2 * (h % 2), 2 * (h % 2) + 1
                act_(T_(o, 0, 512), P(bq), AF.Sigmoid)
                tt('dve', T_(o, 0, 512), T_(o, 0, 512), P(bq), ALU.mult)
                act_(T_(o + 1, 0, 512), P(bz), AF.Sigmoid)
                ts('dve', T_(o + 1, 0, 512), T_(o + 1, 0, 512), col(('omlb', l, d), h), col(('lb', l, d), h), ALU.mult, ALU.add)
                act_(T_(o + 2, 0, 512), T_(o + 1, 0, 512), AF.Ln)
                ts('pool', T_(o + 1, 0, 512), T_(o + 1, 0, 512), -1.0, 1.0, ALU.mult, ALU.add)

            def a2(h):
                o = 5 * (h % 2)
                gla_prep(h, 128, T_(o, 0, 512), T_(o + 1, 0, 512), T_(o + 2, 0, 512), 1.0, rev, 128 ** -0.5, ib=o + 3, ie=o + 4, bufs=bufA)
            pipeline(4, [a0, a1, a2])

            def b_prep():
                v_proj(load_w(wb_in[l], C_VB, 512), VtB)
                slqk = load_w(wb_in[l], C_QB, 512)
                sllr = load_w(wb_in[l], C_LRF, 32)
                proj(sllr, 16 * d, 16, 0)
                cp('act', V(lrb.t[0:16, 0, :], lrb.all().keys), P(0, 0, 512, p=16))

                def b0(h):
                    proj(slqk, h * 64, 64, 0)
                    proj(slqk, 256 + h * 64, 64, 1)
                    mm(P(2, 0, 512, p=64), V(upw.t[0:16, d, h * 64:(h + 1) * 64], upw.all().keys), V(lrb.t[0:16, 0, :], lrb.all().keys), True, True)

                def b1(h):
                    o = 5 * (h % 2)
                    act_(T_(o + 2, 0, 512, 64), P(2, 0, 512, p=64), AF.Exp, scale=-1.0, bias=col(('nupb', l, d), h, p=64))
                    act_(T_(o + 2, 0, 512, 64), T_(o + 2, 0, 512, 64), AF.Ln, scale=1.0, bias=1.0)
                    cp('act', T_(o, 0, 512, 64), P(0, 0, 512, p=64))
                    cp('dve', T_(o + 1, 0, 512, 64), P(1, 0, 512, p=64))

                def b2(h):
                    o = 5 * (h % 2)
                    gla_prep(h, 64, T_(o, 0, 512, 64), T_(o + 1, 0, 512, 64), T_(o + 2, 0, 512, 64), -1.0 / 16.0, rev, 64 ** -0.5, ib=o + 3, ie=o + 4, bufs=bufB, CL=BCL)
                pipeline(4, [b0, b1, b2])
            a_core_ops = record(lambda: gla_core(0, 128, rev, obuf[0], not rev, bufA))
            b_prep_ops = record(b_prep)
            if INTERLEAVE:
                interleave([a_core_ops, b_prep_ops])
            else:
                S.ops.extend(a_core_ops)
                S.ops.extend(b_prep_ops)

            def c_mixer():
                slc = load_w(wb_in[l], C_XC, 512)
                for g in range(4):
                    proj(slc, g * 128, 128, 0, rhs_c0=0, N=512)
                    cp('act', T_(7, 0, 512), P(0))
                    proj(slc, g * 128, 128, 1, rhs_c0=512, N=4)
                    cp('dve', T_(7, 512, 516), P(1, 0, 4))
                    xc = T_(6, 0, 512)
                    act_(xc, T_(7, 0, 512), AF.Identity, scale=col(('ccw', l, 0), g), bias=col(('ccb', l), g))
                    for j in range(1, 4):
                        stt('dve', xc, T_(7, j, j + 512), col(('ccw', l, j), g), xc, ALU.mult, ALU.add)
                    cp('act', xcb.g(g), xc)
                    mm(P(0), wbd.g(d * 8 + g), xcb.g(g), True, True)
                    mm(P(1), wbd.g(d * 8 + 4 + g), xcb.g(g), True, True)
                    sigm_from(T_(0, 0, 512), P(0), T_(1, 0, 512), nbias=col(('nba', l, d), g))
                    sigm_from(T_(2, 0, 512), P(1), T_(1, 0, 512), nbias=col(('nbx', l, d), g))
                    act_(T_(4, 0, 512), T_(0, 0, 512), AF.Exp, scale=col(('clam', l, d), g))
                    act_(T_(5, 0, 512), T_(0, 0, 512), AF.Exp, scale=col(('clam2', l, d), g))
                    ts('pool', T_(5, 0, 512), T_(5, 0, 512), -1.0, 1.0, ALU.mult, ALU.add)
                    act_(T_(5, 0, 512), T_(5, 0, 512), AF.Ln)
                    act_(T_(5, 0, 512), T_(5, 0, 512), AF.Exp, scale=0.5)
                    tt('pool', T_(2, 0, 512), T_(2, 0, 512), xc, ALU.mult)
                    tt('dve', T_(2, 0, 512), T_(2, 0, 512), T_(5, 0, 512), ALU.mult)
                    hs = V(hst.t[:, g, d:d + 1], [('hst', g)])
                    if rev:
                        scan(V(tmp[1].t[:, 0, 511::-1], tmp[1].all().keys),
                             V(tmp[4].t[:, 0, 511::-1], tmp[4].all().keys), V(tmp[2].t[:, 0, 511::-1], tmp[2].all().keys), hs)
                        cp('dve', hs, T_(1, 0, 1))
                        cp('act', obuf[2].g(g), T_(1, 0, 512))
                    else:
                        scan(T_(1, 0, 512), T_(4, 0, 512), T_(2, 0, 512), hs)
                        cp('dve', hs, T_(1, 511, 512))
                        tt('pool', obuf[2].g(g), T_(1, 0, 512), obuf[2].g(g), ALU.add)


            b_core_ops = record(lambda: gla_core(1, 64, rev, obuf[1], not rev, bufB, CL=BCL))
            c_ops = record(c_mixer)
            if INTERLEAVE:
                interleave([b_core_ops, c_ops])
            else:
                S.ops.extend(b_core_ops)
                S.ops.extend(c_ops)
            if rev:
                for b in range(3):
                    dma('q_pool', V(obwd[b * 512:(b + 1) * 512, n * TT:(n + 1) * TT].rearrange("(g p) c -> p g c", p=128), ['obwd']),
                        obuf[b].all(), key=('sto', b))
                continue
            BK3 = ((0, 1), (2, 3), (5, 6))
            for b in range(2):
                if n_mix < b + 1:
                    memset('pool', obf[b].all(), 0.0)
                else:
                    act_(V(sqb.t[:, 4 * b:4 * b + 4, 0:512], [('sqb', 4 * b + q) for q in range(4)]), obuf[b].all(), AF.Square)
            slog = {}
            pcfg = ((C_OGA, 'hnw'), (C_OGB, 'gnw'))

            def pa0(u):
                b, h = u // 4, u % 4
                mm(P(BK3[u % 3][0]), onesb.all(), sqb.g(4 * b + h, 0, 512), True, True)

            def pa1(u):
                ta = T_(8 + u % 3, 0, 512)
                act_(ta, P(BK3[u % 3][0]), AF.Ln, scale=1.0 / 128.0, bias=EPS)
                act_(ta, ta, AF.Exp, scale=-0.5)

            def pa2(u):
                b, h = u // 4, u % 4
                stt('dve', T_(u, 0, 512), obuf[b].g(h), col((pcfg[b][1], l)), T_(8 + u % 3, 0, 512), ALU.mult, ALU.mult)
            pipeline(8, [pa0, pa1, pa2])

            def pb0(u):
                b, h = u // 4, u % 4
                if h == 0:
                    slog[b] = load_w(wb_in[l], pcfg[b][0], 512)
                proj(slog[b], h * 128, 128, BK3[u % 3][1])

            def pb1(u):
                act_(T_(8 + u % 3, 0, 512), P(BK3[u % 3][1]), AF.Silu)

            def pb2(u):
                b, h = u // 4, u % 4
                tt('dve', obf[b].g(h), T_(u, 0, 512), T_(8 + u % 3, 0, 512), ALU.mult)
            pipeline(8, [pb0, pb1, pb2])
            if n_mix >= 3:
                sly = load_w(wb_in[l], C_YC, 512)

                def y0(g):
                    proj(sly, g * 128, 128, g % 2)

                def y1(g):
                    act_(T_(g % 3, 0, 512), P(g % 2), AF.Gelu_apprx_tanh)

                def y2(g):
                    tt('dve', obf[2].g(g), T_(g % 3, 0, 512), obuf[2].g(g), ALU.mult)
                pipeline(4, [y0, y1, y2])
            else:
                memset('pool', obf[2].all(), 0.0)
            BK = ((0, 1), (2, 3), (5, 6))
            wl = {}

            def mg0(u):
                gc, br, gg = u // 12, (u % 12) // 4, u % 4
                if gg == 0:
                    wl[(gc, br)] = (load_w(wb_in[l], C_GA + br * 1024 + gc * 512, 512),
                                    load_w(wb_br[l, br * 512:(br + 1) * 512, :], gc * 512, 512, kg=4))
                slg, slb = wl[(gc, br)]
                proj(slg, gg * 128, 128, BK[u % 3][0])
                proj(slb, gg * 128, 128, BK[u % 3][1], rhs_c0=0, kg=4, rhs=obf[br])

            def mg1(u):
                act_(T_(u % 3, 0, 512), P(BK[u % 3][0]), AF.Sigmoid)

            def mg2(u):
                gc, br, gg = u // 12, (u % 12) // 4, u % 4
                g = gc * 4 + gg
                macc = T_(3 + g, 0, 512)
                sg = T_(u % 3, 0, 512)
                if br == 0:
                    tt('dve', macc, sg, P(BK[u % 3][1]), ALU.mult)
                else:
                    tt('dve', sg, sg, P(BK[u % 3][1]), ALU.mult)
                    tt('pool', macc, macc, sg, ALU.add)
                if br == 2:
                    cp('act', mbf.g(g), macc)
            pipeline(24, [mg0, mg1, mg2])
            wo = {}

            def wo0(u):
                if u % 4 == 0:
                    wo[u // 4] = load_w(wb_out[l], (u // 4) * 512, 512)
                dma('q_pool', T_(3 + u, 0, 512), V(xA[u * 128:(u + 1) * 128, n * TT + 2:n * TT + 2 + TT], ['xA']), key=('ldr', u % 4))
                proj(wo[u // 4], (u % 4) * 128, 128, u % 2, rhs_c0=0, rhs=mbf)

            def wo1(u):
                tt('dve', T_(3 + u, 0, 512), P(u % 2), T_(3 + u, 0, 512), ALU.add)
                dma('q_pool', V(xB[u * 128:(u + 1) * 128, n * TT + 2:n * TT + 2 + TT], ['xB']), T_(3 + u, 0, 512), key=('str', u % 4))
            pipeline(8, [wo0, wo1])

    def gelu_mul(outv, cg, cv, t1):
        act_(t1, cg, AF.Square)
        ts('dve', t1, t1, 0.044715, 1.0, ALU.mult, ALU.add)
        tt('dve', t1, t1, cg, ALU.mult)
        act_(t1, t1, AF.Relu, scale=1.5957691216057308, bias=40.0)
        act_(t1, t1, AF.Exp, scale=-1.0, bias=40.0)
        act_(t1, t1, AF.Ln, scale=1.0, bias=1.0)
        act_(t1, t1, AF.Exp, scale=-1.0)
        tt('pool', t1, t1, cg, ALU.mult)
        tt('dve', outv, t1, cv, ALU.mult)

    def FT(k, c0=None, c1=None):
        if k < 10:
            return T_(k, c0, c1)
        return obuf[(k - 10) // 4].g((k - 10) % 4, c0, c1)

    def ffn_sweep(l):
        def wlen(ww):
            return min(510, T - 510 * ww) + 2

        def ldxw(ww):
            W_ = wlen(ww)
            dma('q_pool', V(xt.t[:, :, 0:W_], xt.all().keys), V(xB[:, 510 * ww + 1:510 * ww + 1 + W_].rearrange("(g p) c -> p g c", p=128), ['xB']), key='ldx')
        for w in range(NWIN):
            c0 = 510 * w + 1
            nout = min(510, T - 510 * w)
            WL = nout + 2
            if w == 0:
                ldxw(0)
            rmsnorm(('nffn', l), 0, WL)
            if w + 1 < NWIN:
                ldxw(w + 1)
            wl = {}
            FB = ((0, 1), (2, 3), (5, 6))

            def f0(j):
                jb, gg = j // 4, j % 4
                if gg == 0:
                    ng = 4 if jb < 5 else 2
                    wl[jb] = (load_w(wb_up[l], jb * 512, ng * 128), load_w(wb_up[l], DFF + jb * 512, ng * 128))
                proj(wl[jb][0], gg * 128, 128, FB[j % 3][0], rhs_c0=0, N=WL)
                proj(wl[jb][1], gg * 128, 128, FB[j % 3][1], rhs_c0=0, N=WL)

            def f1(j):
                for q, fi in ((0, j), (1, 22 + j)):
                    bk = FB[j % 3][q]
                    C = FT(4 + 2 * (j % 4) + q, 1, WL - 1)
                    act_(C, P(bk, 0, WL - 2), AF.Identity, scale=col(('fcw', l, 0), fi), bias=col(('fcb', l), fi))
                    stt('dve', C, P(bk, 1, WL - 1), col(('fcw', l, 1), fi), C, ALU.mult, ALU.add)
                    stt('dve', C, P(bk, 2, WL), col(('fcw', l, 2), fi), C, ALU.mult, ALU.add)

            def f2(j):
                cg = FT(4 + 2 * (j % 4), 1, WL - 1)
                t1 = FT(12 + j % 3, 1, WL - 1)
                act_(t1, cg, AF.Gelu_apprx_tanh)

            def f3(j):
                cv = FT(4 + 2 * (j % 4) + 1, 1, WL - 1)
                t1 = FT(12 + j % 3, 1, WL - 1)
                tt('dve', actg(j, 0, nout), t1, cv, ALU.mult)
            pipeline(22, [f0, f1, f2, f3])
            for oc in range(2):
                for kb in range(3):
                    nk = 8 if kb < 2 else 6
                    sd = load_w(wb_dn[l, kb * 1024:kb * 1024 + nk * 128, :], oc * 512, 512, kg=nk)
                    for gg in range(4):
                        for kk in range(nk):
                            j = kb * 8 + kk
                            mm(P(gg, 0, nout), V(sd.t[:, kk, gg * 128:(gg + 1) * 128], sd.all().keys), actg(j, 0, nout), j == 0, j == 21)
                for gg in range(4):
                    g = oc * 4 + gg
                    r = lambda a, b: FT(15 + gg, a, b)
                    dma('q_pool', r(0, nout), V(xB[g * 128:(g + 1) * 128, c0 + 1:c0 + 1 + nout], ['xB']), key=('ldr', gg))
                    tt('dve', r(0, nout), P(gg, 0, nout), r(0, nout), ALU.add)
                    dma('q_pool', V(xA[g * 128:(g + 1) * 128, 510 * w + 2:510 * w + 2 + nout], ['xA']), r(0, nout), key=('str', gg))

    for l in range(L):
        dma('sp', upwf.all(), V(gla_up_w[l].rearrange("d r c -> r d c"), []), key='upw')
        cp('dve', upw.all(), upwf.all())
        for d in range(2):
            for ax, wsrc in ((0, rglru_wa), (1, rglru_wx)):
                stg = V(tmp[10].t[:, 0, 0:512].rearrange("p (g c) -> p g c", g=4), tmp[10].all().keys)
                memset('pool', stg, 0.0)
                for g in range(4):
                    for hh in range(2):
                        dma('sp', V(tmp[10].t[hh * 64:(hh + 1) * 64, 0, g * 128 + hh * 64:g * 128 + hh * 64 + 64], tmp[10].all().keys),
                            V(wsrc[l, d, 2 * g + hh], []), key=('wbd', 2 * g + hh))
                cp('dve', V(wbd.t[:, d * 8 + ax * 4:d * 8 + ax * 4 + 4, :], [('wbd', d * 8 + ax * 4 + q) for q in range(4)]), stg)
        mixer_sweep(l, True)
        mixer_sweep(l, False)
        ffn_sweep(l)

    for n in range(NT):
        dma('q_pool', V(xt.t[:, :, 0:512], xt.all().keys), V(xA[:, n * TT + 2:n * TT + 2 + TT].rearrange("(g p) c -> p g c", p=128), ['xA']), key='ldx')
        rmsnorm('fnw', 0, 512, outf=lambda g: obuf[1 + g // 4].g(g % 4))
        for s in range(4):
            for half in range(2):
                pb = 5 + half
                for gg in range(4):
                    g = half * 4 + gg
                    tr(P(pb, gg * 128, gg * 128 + 128), obuf[1 + g // 4].g(g % 4, s * 128, s * 128 + 128), identf.all())
                cp('act' if half == 0 else 'dve', V(obuf[0].t[:, 2 * (s % 2) + half, :], [('obuf0', 2 * (s % 2) + half)]), P(pb))
            dma('sp', V(y_out[n * TT + s * 128:n * TT + (s + 1) * 128, :], ['y']),
                V(obuf[0].t[:, 2 * (s % 2):2 * (s % 2) + 2, :].rearrange("p g c -> p (g c)"), [('obuf0', 2 * (s % 2)), ('obuf0', 2 * (s % 2) + 1)]), key=('sty', s % 2))
    st = S.emit()
    return nc, st


_CACHE = {}
WNAMES = ["norm_mix_w", "w_in", "hgrn_lb_logits", "hgrn_norm_w", "gla_up_w", "gla_up_b", "gla_norm_w", "c_conv_w",
          "c_conv_b", "rglru_wa", "rglru_ba", "rglru_wx", "rglru_bx", "rglru_lam", "w_branch", "w_out", "norm_ffn_w",
          "ffn_up", "ffn_conv_w", "ffn_conv_b", "ffn_down", "final_norm_w"]


def kernel(x_prompt, x_sample, **w):
    x_prompt = np.asarray(x_prompt, np.float32)
    x_sample = np.asarray(x_sample, np.float32)
    T = x_prompt.shape[1]
    L = np.asarray(w["w_in"]).shape[0]
    seqs = [x_prompt[i] for i in range(x_prompt.shape[0])] + [x_sample[i] for i in range(x_sample.shape[0])]
    key = (T, L)
    if key not in _CACHE:
        _CACHE[key] = build(T, L)[0]
    nc = _CACHE[key]
    wd = {k: np.ascontiguousarray(np.asarray(w[k], np.float32)) for k in WNAMES}
    in_maps = []
    for c in range(8):
        m = dict(wd)
        m["x"] = np.ascontiguousarray(seqs[c]) if c < len(seqs) else np.zeros((T, D), np.float32)
        in_maps.append(m)
    res = run_bass_kernel_spmd(nc, in_maps, core_ids=list(range(8)))
    ys = [np.asarray(res.results[c]["y"], np.float32) for c in range(len(seqs))]
    nb = x_prompt.shape[0]
    return (np.stack(ys[:nb], 0), np.stack(ys[nb:], 0))
```

```python
import numpy as np
import concourse.bass as bass
import concourse.mybir as mybir
from concourse.bass_utils import run_bass_kernel_spmd

F32 = mybir.dt.float32
BF16 = mybir.dt.bfloat16
AF = mybir.ActivationFunctionType
ALU = mybir.AluOpType

D = 1024
NIN = 8224
DFF = 2816
TT = 512
import os
INTERLEAVE = not os.environ.get('NOINT')
BCL = 128
EPS = 1e-6
C_QA, C_ZF, C_ZB, C_IA, C_OGA = 0, 512, 1024, 1536, 2048
C_QB, C_KB, C_VB, C_OGB, C_LRF, C_LRB = 2560, 2816, 3072, 3584, 4096, 4112
C_XC, C_YC, C_GA = 4128, 4640, 5152


class Sched:
    def __init__(self, nc):
        self.nc = nc
        self.ops = []
        self.engs = {'pe': nc.tensor, 'act': nc.scalar, 'dve': nc.vector, 'pool': nc.gpsimd, 'sp': nc.sync}

    def op(self, eng, fn, reads=(), writes=()):
        self.ops.append((eng, fn, tuple(reads), tuple(writes), None))

    def dma(self, q, fn, reads=(), writes=(), key=None):
        self.ops.append((q, fn, tuple(reads), tuple(writes), key))

    def emit(self):
        nc = self.nc
        ops = self.ops
        n = len(ops)
        writers, readers = {}, {}
        deps = [None] * n
        signal = [False] * n
        for i, (eng, fn, reads, writes, key) in enumerate(ops):
            stream = ('dma', key) if key is not None else eng
            d = set()
            for r in reads:
                w = writers.get(r)
                if w:
                    d.update(w.values())
            for wr in writes:
                w = writers.get(wr)
                if w:
                    d.update(w.values())
                rd = readers.get(wr)
                if rd:
                    d.update(rd.values())
            dl = []
            for j in d:
                if ops[j][4] is None and ops[j][0] == 'pe' and eng == 'pe' and key is None:
                    continue
                dl.append(j)
                signal[j] = True
            deps[i] = dl
            for r in reads:
                readers.setdefault(r, {})[stream] = i
            for wr in writes:
                writers.setdefault(wr, {})[stream] = i
        sems = {}

        def sem_for(name):
            if name not in sems:
                sems[name] = nc.alloc_semaphore('s_%d' % len(sems))
            return sems[name]
        count = {}
        sigval = [None] * n
        seen = {}
        nmap = {'q_act': 'act', 'q_pool': 'pool'}
        for i, (eng, fn, reads, writes, key) in enumerate(ops):
            ename = nmap.get(eng, eng)
            e = self.engs[ename]
            sn = seen.setdefault(ename, {})
            need = {}
            for j in deps[i]:
                s, v = sigval[j]
                if need.get(s, 0) < v:
                    need[s] = v
            for s, v in need.items():
                if sn.get(s, 0) >= v:
                    continue
                e.wait_ge(sem_for(s), v)
                sn[s] = v
            if key is not None:
                s = ('dma', key)
                pv = count.get(s, 0)
                if pv and sn.get(s, 0) < pv:
                    e.wait_ge(sem_for(s), pv)
                    sn[s] = pv
            inst = fn(e)
            if key is not None:
                s = ('dma', key)
                count[s] = count.get(s, 0) + 16
                inst.then_inc(sem_for(s), 16)
                sigval[i] = (s, count[s])
            elif signal[i]:
                s = ename
                count[s] = count.get(s, 0) + 1
                inst.then_inc(sem_for(s), 1)
                sigval[i] = (s, count[s])
        sp = self.engs['sp']
        for s, v in count.items():
            if isinstance(s, tuple):
                sp.wait_ge(sem_for(s), v)
        return dict(n_ops=n, n_sems=len(sems))


class V:
    def __init__(self, ap, keys):
        self.ap = ap
        self.keys = list(keys)


class Buf:
    def __init__(self, nc, name, shape, dtype):
        self.t = nc.alloc_sbuf_tensor(name, list(shape), dtype)
        self.name = name
        self.shape = shape

    def g(self, i, c0=None, c1=None):
        return V(self.t[:, i, c0:c1], [(self.name, i)])

    def gp(self, i, p, c0=None, c1=None):
        return V(self.t[0:p, i, c0:c1], [(self.name, i)])

    def all(self):
        n = self.shape[1] if len(self.shape) > 2 else 1
        return V(self.t[:], [(self.name, i) for i in range(n)])

    def v(self, ap_fn, idxs):
        return V(ap_fn(self.t), [(self.name, i) for i in idxs])


def build(T, L, n_mix=3):
    assert T % TT == 0
    NT = T // TT
    XW = T + 512
    NWIN = (T + 509) // 510
    nc = bass.Bass("TRN2", target_bir_lowering=False)
    S = Sched(nc)

    def din(name, shape):
        return nc.dram_tensor(name, list(shape), F32, kind="ExternalInput").ap()
    x_in = din("x", [T, D])
    norm_mix_w = din("norm_mix_w", [L, D])
    w_in = din("w_in", [L, D, NIN])
    lb_logits = din("hgrn_lb_logits", [4, 2, 512])
    hgrn_norm_w = din("hgrn_norm_w", [L, 128])
    gla_up_w = din("gla_up_w", [L, 2, 16, 256])
    gla_up_b = din("gla_up_b", [L, 2, 256])
    gla_norm_w = din("gla_norm_w", [L, 128])
    c_conv_w = din("c_conv_w", [L, 4, 512])
    c_conv_b = din("c_conv_b", [L, 512])
    rglru_wa = din("rglru_wa", [L, 2, 8, 64, 64])
    rglru_ba = din("rglru_ba", [L, 2, 512])
    rglru_wx = din("rglru_wx", [L, 2, 8, 64, 64])
    rglru_bx = din("rglru_bx", [L, 2, 512])
    rglru_lam = din("rglru_lam", [L, 2, 512])
    w_branch = din("w_branch", [L, 3, 512, D])
    w_out = din("w_out", [L, D, D])
    norm_ffn_w = din("norm_ffn_w", [L, D])
    ffn_up = din("ffn_up", [L, D, 2 * DFF])
    ffn_conv_w = din("ffn_conv_w", [L, 3, 2 * DFF])
    ffn_conv_b = din("ffn_conv_b", [L, 2 * DFF])
    ffn_down = din("ffn_down", [L, DFF, D])
    final_norm_w = din("final_norm_w", [D])
    y_out = nc.dram_tensor("y", [T, D], F32, kind="ExternalOutput").ap()

    def dscr(name, shape, dt):
        return nc.dram_tensor(name, list(shape), dt, kind="Internal").ap()
    xA = dscr("xA", [D, XW], F32)
    xB = dscr("xB", [D, XW], F32)
    obwd = dscr("obwd", [1536, T], F32)
    wb_in = dscr("wb_in", [L, D, NIN], BF16)
    wb_br = dscr("wb_br", [L, 1536, D], BF16)
    wb_out = dscr("wb_out", [L, D, D], BF16)
    wb_up = dscr("wb_up", [L, D, 2 * DFF], BF16)
    wb_dn = dscr("wb_dn", [L, DFF, D], BF16)

    ps = [nc.alloc_psum_tensor("ps%d" % i, [128, 512], F32) for i in range(7)]
    psT = nc.alloc_psum_tensor("psT", [128, 1024], BF16)

    def P(i, c0=None, c1=None, p=128):
        return V(ps[i][0:p, c0:c1], [('ps', i)])

    xt = Buf(nc, "xt", [128, 8, 516], F32)
    sqb = Buf(nc, "sqb", [128, 8, 516], BF16)
    hb = Buf(nc, "hb", [128, 8, 516], BF16)
    rstd = Buf(nc, "rstd", [128, 1, 516], F32)
    NWS = 3
    wsl = [Buf(nc, "wsl%d" % i, [128, 8, 512], BF16) for i in range(NWS)]
    tmp = [Buf(nc, "tmp%d" % i, [128, 1, 516], F32) for i in range(11)]
    qd = Buf(nc, "qd", [128, 4, 512], BF16)
    ki = Buf(nc, "ki", [128, 4, 512], BF16)
    ktl = Buf(nc, "ktl", [128, 4, 512], BF16)
    ktm = [Buf(nc, "ktm%d" % i, [128, 4, 512], BF16) for i in range(2)]
    Vt = Buf(nc, "Vt", [128, 4, 512], BF16)
    qdB = Buf(nc, "qdB", [128, 4, 512], BF16)
    kiB = Buf(nc, "kiB", [128, 4, 512], BF16)
    ktlB = Buf(nc, "ktlB", [128, 4, 512], BF16)
    VtB = Buf(nc, "VtB", [128, 4, 512], BF16)
    lamB = Buf(nc, "lamB", [128, 4, 16], F32)
    attM = [Buf(nc, "attM%d" % i, [128, 1, 512], BF16) for i in range(2)]
    lam = Buf(nc, "lam", [128, 4, 16], F32)
    Scur = [Buf(nc, "Scur%d" % i, [128, 4, 128], F32) for i in range(2)]
    Sbf = [Buf(nc, "Sbf%d" % i, [128, 4, 640], BF16) for i in range(2)]
    obuf = [Buf(nc, "obuf%d" % i, [128, 4, 512], F32) for i in range(3)]
    obf = [Buf(nc, "obf%d" % i, [128, 4, 512], BF16) for i in range(3)]
    xcb = Buf(nc, "xcb", [128, 4, 512], BF16)
    hst = Buf(nc, "hst", [128, 4, 2], F32)
    mbf = Buf(nc, "mbf", [128, 8, 512], BF16)
    wbd = Buf(nc, "wbd", [128, 16, 128], BF16)
    upw = Buf(nc, "upw", [16, 2, 256], BF16)
    upwf = Buf(nc, "upwf", [16, 2, 256], F32)
    lrb = Buf(nc, "lrb", [16, 1, 512], BF16)
    NCST = 1600
    cst = Buf(nc, "cst", [128, 1, NCST], F32)
    identf = Buf(nc, "identf", [128, 1, 128], F32)
    identb = Buf(nc, "identb", [128, 1, 128], BF16)
    onesb = Buf(nc, "onesb", [128, 1, 128], BF16)
    zerob = Buf(nc, "zerob", [128, 1, 512], BF16)
    maskF = Buf(nc, "maskF", [128, 1, 128], F32)
    maskB = Buf(nc, "maskB", [128, 1, 128], F32)
    cmask = Buf(nc, "cmask", [128, 1, 4], F32)
    maskF1 = Buf(nc, "maskF1", [128, 1, 128], F32)
    maskB1 = Buf(nc, "maskB1", [128, 1, 128], F32)
    rmF = Buf(nc, "rmF", [128, 1, 512], F32)
    rmB = Buf(nc, "rmB", [128, 1, 512], F32)

    def actg(j, c0=None, c1=None):
        if j < 12:
            return obf[j // 4].g(j % 4, c0, c1)
        if j < 20:
            return mbf.g(j - 12, c0, c1)
        return xcb.g(j - 20, c0, c1)

    def k_(*vs):
        r = []
        for v in vs:
            if isinstance(v, V):
                r += v.keys
        return r

    def act_(out, in_, func, scale=1.0, bias=0.0):
        sc = scale.ap if isinstance(scale, V) else scale
        bi = bias.ap if isinstance(bias, V) else bias
        S.op('act', lambda e: e.activation(out=out.ap, in_=in_.ap, func=func, bias=bi, scale=sc),
             reads=k_(in_, scale, bias), writes=out.keys)

    def tt(eng, out, a, b, op):
        S.op(eng, lambda e: e.tensor_tensor(out=out.ap, in0=a.ap, in1=b.ap, op=op), reads=k_(a, b), writes=out.keys)

    def ts(eng, out, a, s1, s2, op0, op1=None):
        s1a = s1.ap if isinstance(s1, V) else s1
        s2a = s2.ap if isinstance(s2, V) else s2
        if op1 is None:
            S.op(eng, lambda e: e.tensor_scalar(out=out.ap, in0=a.ap, scalar1=s1a, scalar2=None, op0=op0),
                 reads=k_(a, s1), writes=out.keys)
        else:
            S.op(eng, lambda e: e.tensor_scalar(out=out.ap, in0=a.ap, scalar1=s1a, scalar2=s2a, op0=op0, op1=op1),
                 reads=k_(a, s1, s2), writes=out.keys)

    def stt(eng, out, a, sc, b, op0, op1):
        sca = sc.ap if isinstance(sc, V) else sc
        S.op(eng, lambda e: e.scalar_tensor_tensor(out=out.ap, in0=a.ap, scalar=sca, in1=b.ap, op0=op0, op1=op1),
             reads=k_(a, sc, b), writes=out.keys)

    def cp(eng, out, a):
        if eng == 'act':
            act_(out, a, AF.Copy)
        else:
            S.op(eng, lambda e: e.tensor_copy(out=out.ap, in_=a.ap), reads=a.keys, writes=out.keys)

    def rcp(out, a):
        S.op('dve', lambda e: e.reciprocal(out=out.ap, in_=a.ap), reads=a.keys, writes=out.keys)

    def scan(out, d0, d1, init):
        ia = init.ap if isinstance(init, V) else init
        S.op('dve', lambda e: e.tensor_tensor_scan(out=out.ap, data0=d0.ap, data1=d1.ap, initial=ia,
                                                    op0=ALU.mult, op1=ALU.add),
             reads=k_(d0, d1, init), writes=out.keys)

    def mm(out, lhsT, rhs, start, stop):
        S.op('pe', lambda e: e.matmul(out.ap, lhsT=lhsT.ap, rhs=rhs.ap, start=start, stop=stop, skip_group_check=True),
             reads=k_(lhsT, rhs), writes=out.keys)

    def tr(out, in_, ident):
        S.op('pe', lambda e: e.transpose(out=out.ap, in_=in_.ap, identity=ident.ap), reads=k_(in_, ident), writes=out.keys)

    def memset(eng, out, val):
        S.op(eng, lambda e: e.memset(out.ap, val), writes=out.keys)

    def dma(q, out, in_, key, slow=False):
        S.dma(q, lambda e: e.dma_start(out=out.ap, in_=in_.ap, allow_slow_non_contiguous=slow),
              reads=in_.keys, writes=out.keys, key=key)

    def asel(out, pattern, cmp, base, cm):
        S.op('pool', lambda e: e.affine_select(out=out.ap, in_=out.ap, pattern=pattern, compare_op=cmp, fill=0.0,
                                               base=base, channel_multiplier=cm), reads=out.keys, writes=out.keys)

    memset('pool', identf.all(), 1.0)
    asel(identf.all(), [[-1, 128]], ALU.is_equal, 0, 1)
    cp('dve', identb.all(), identf.all())
    memset('pool', onesb.all(), 1.0)
    memset('pool', zerob.all(), 0.0)
    memset('pool', tmp[10].all(), 0.0)
    for mk, pat, cm in ((maskF, [[1, 128]], -1), (maskB, [[-1, 128]], 1)):
        memset('pool', mk.all(), 1.0)
        asel(mk.all(), pat, ALU.is_ge, 0, cm)
        for c in range(4):
            blk = mk.g(0, 32 * c, 32 * c + 32)
            asel(blk, [[0, 32]], ALU.is_ge, -32 * c, 1)
            asel(blk, [[0, 32]], ALU.is_ge, 32 * c + 31, -1)
    for mk, pat, cm in ((maskF1, [[1, 128]], -1), (maskB1, [[-1, 128]], 1)):
        memset('pool', mk.all(), 1.0)
        asel(mk.all(), pat, ALU.is_ge, 0, cm)
    memset('pool', cmask.all(), 1.0)
    for c in range(4):
        blk = cmask.g(0, c, c + 1)
        asel(blk, [[0, 1]], ALU.is_ge, -32 * c, 1)
        asel(blk, [[0, 1]], ALU.is_ge, 32 * c + 31, -1)
    memset('pool', rmF.all(), 1.0)
    memset('pool', rmB.all(), 1.0)
    memset('pool', rmF.v(lambda t: t[:, 0, 0:512:32], [0]), 0.0)
    memset('pool', rmB.v(lambda t: t[:, 0, 31:512:32], [0]), 0.0)

    ccol = [0]
    cmap = {}

    def newcols(name, n):
        c0 = ccol[0]
        ccol[0] += n
        assert ccol[0] <= NCST, ccol[0]
        cmap[name] = c0
        return c0

    def load_cols(name, ap1d, n, p=128):
        c0 = newcols(name, n)
        dma('sp', V(cst.t[0:p, 0, c0:c0 + n], [('cst', name)]),
            V(ap1d.rearrange("(g p) -> p g", p=p), []), key=('cst', c0 % 8), slow=True)

    def cols(name, i0=0, n=1, p=128):
        c = cmap[name] + i0
        return V(cst.t[0:p, 0, c:c + n], [('cst', name)])

    def col(name, i=0, p=128):
        return cols(name, i, 1, p)

    for l in range(L):
        load_cols(('nmix', l), norm_mix_w[l], 8)
        load_cols(('nffn', l), norm_ffn_w[l], 8)
        load_cols(('hnw', l), hgrn_norm_w[l], 1)
        load_cols(('gnw', l), gla_norm_w[l], 1)
        for d in range(2):
            load_cols(('upb', l, d), gla_up_b[l, d], 4, p=64)
            load_cols(('ba', l, d), rglru_ba[l, d], 4)
            load_cols(('bx', l, d), rglru_bx[l, d], 4)
            load_cols(('lam', l, d), rglru_lam[l, d], 4)
        for j in range(4):
            load_cols(('ccw', l, j), c_conv_w[l, j], 4)
        load_cols(('ccb', l), c_conv_b[l], 4)
        for j in range(3):
            load_cols(('fcw', l, j), ffn_conv_w[l, j], 44)
        load_cols(('fcb', l), ffn_conv_b[l], 44)
    load_cols('fnw', final_norm_w, 8)
    for ll in range(4):
        for d in range(2):
            load_cols(('lbl', ll, d), lb_logits[ll, d], 4)
    for l in range(L):
        for d in range(2):
            for nm in ('upb', 'ba', 'bx'):
                newcols(('n' + nm, l, d), 4)
                p = 64 if nm == 'upb' else 128
                ts('dve', cols(('n' + nm, l, d), 0, 4, p), cols((nm, l, d), 0, 4, p), -1.0, None, ALU.mult)
            newcols(('clam', l, d), 4)
            newcols(('clam2', l, d), 4)
            c1 = cols(('clam', l, d), 0, 4)
            act_(c1, cols(('lam', l, d), 0, 4), AF.Exp, scale=-1.0)
            act_(c1, c1, AF.Ln, scale=1.0, bias=1.0)
            ts('dve', cols(('clam2', l, d), 0, 4), c1, -16.0, None, ALU.mult)
            ts('dve', c1, c1, -8.0, None, ALU.mult)
    for d in range(2):
        newcols(('lbe', d), 16)
        for ll in range(4):
            act_(cols(('lbe', d), 4 * ll, 4), cols(('lbl', ll, d), 0, 4), AF.Exp)
        newcols(('lbs', d), 4)
        sm = cols(('lbs', d), 0, 4)
        tt('dve', sm, cols(('lbe', d), 0, 4), cols(('lbe', d), 4, 4), ALU.add)
        tt('dve', sm, sm, cols(('lbe', d), 8, 4), ALU.add)
        tt('dve', sm, sm, cols(('lbe', d), 12, 4), ALU.add)
        rcp(sm, sm)
        for ll in range(4):
            newcols(('lb', ll, d), 4)
            newcols(('omlb', ll, d), 4)
            lbv = cols(('lb', ll, d), 0, 4)
            if ll == 0:
                memset('dve', lbv, 0.0)
            else:
                tt('dve', lbv, cols(('lbe', d), 4 * ll, 4), sm, ALU.mult)
                tt('dve', lbv, lbv, cols(('lb', ll - 1, d), 0, 4), ALU.add)
            ts('dve', cols(('omlb', ll, d), 0, 4), lbv, -1.0, 1.0, ALU.mult, ALU.add)

    for xs, nm in ((xA, 'xA'), (xB, 'xB')):
        for g in range(8):
            dma('sp', V(xs[g * 128:(g + 1) * 128, T + 2:XW], [nm]), V(tmp[10].t[:, 0, 0:510], tmp[10].all().keys), key=('zp', g))
            dma('sp', V(xs[g * 128:(g + 1) * 128, 0:2], [nm]), V(tmp[10].t[:, 0, 0:2], tmp[10].all().keys), key=('zp', g))

    cv_i = [0]
    ceng = ['act', 'dve']

    def conv_w(src2d, dst2d, R, C):
        cw = 2056 if C == NIN else min(C, 1408)
        assert C % cw == 0
        for r in range(R // 128):
            for c in range(C // cw):
                i = cv_i[0]
                cv_i[0] += 1
                s = i % 2
                fv = V(xt.t[:, 4 * s:4 * s + 4, :].rearrange("p g c -> p (g c)")[:, 0:cw], [('xt', 4 * s + q) for q in range(4)])
                bv = V(sqb.t[:, 4 * s:4 * s + 4, :].rearrange("p g c -> p (g c)")[:, 0:cw], [('sqb', 4 * s + q) for q in range(4)])
                dma('sp', fv, V(src2d[r * 128:(r + 1) * 128, c * cw:(c + 1) * cw], []), key=('cvf', s))
                cp(ceng[i % 2], bv, fv)
                dma('q_pool', V(dst2d[r * 128:(r + 1) * 128, c * cw:(c + 1) * cw], ['wbf']), bv, key=('cvb', s))
    for l in range(L):
        conv_w(w_in[l], wb_in[l], D, NIN)
        conv_w(w_branch[l].rearrange("b k d -> (b k) d"), wb_br[l], 1536, D)
        conv_w(w_out[l], wb_out[l], D, D)
        conv_w(ffn_up[l], wb_up[l], D, 2 * DFF)
        conv_w(ffn_down[l], wb_dn[l], DFF, D)

    def tokv(s, c0=None, c1=None):
        b = obuf[s % 2]
        return V(b.t[:, 0:2, :].rearrange("p g c -> p (g c)")[:, c0:c1], [(b.name, 0), (b.name, 1)])
    for n in range(NT):
        for s in range(4):
            dma('sp', tokv(s), V(x_in[n * TT + s * 128:n * TT + (s + 1) * 128, :], []), key=('tok', s % 2))
            for half in range(2):
                pb = 4 + half
                for gg in range(4):
                    g = half * 4 + gg
                    tr(P(pb, gg * 128, gg * 128 + 128), tokv(s, g * 128, g * 128 + 128), identf.all())
                outv = V(xt.t[:, half * 4:half * 4 + 4, 2 + s * 128:2 + s * 128 + 128], [('xt', half * 4 + gg) for gg in range(4)])
                inv = V(ps[pb][:, :].rearrange("p (g c) -> p g c", g=4), [('ps', pb)])
                cp('act' if half == 0 else 'dve', outv, inv)
        dma('q_pool', V(xA[:, n * TT + 2:n * TT + 2 + TT].rearrange("(g p) c -> p g c", p=128), ['xA']),
            V(xt.t[:, :, 2:514], xt.all().keys), key='stx')

    wrot = [0]

    def load_w(src2d, c0, w, kg=8):
        s = wrot[0] % NWS
        wrot[0] += 1
        sl = wsl[s]
        dma('sp', V(sl.t[:, 0:kg, 0:w], sl.all().keys),
            V(src2d[:, c0:c0 + w].rearrange("(k p) c -> p k c", p=128), ['wbf']), key=('w', s))
        return sl

    def rmsnorm(wname, c0, c1, outf=None):
        act_(V(sqb.t[:, :, c0:c1], sqb.all().keys), V(xt.t[:, :, c0:c1], xt.all().keys), AF.Square)
        for (a, b) in ((c0, min(c1, c0 + 512)), (c0 + 512, c1)):
            if b <= a:
                continue
            for g in range(8):
                mm(P(4, 0, b - a), onesb.all(), sqb.g(g, a, b), g == 0, g == 7)
            r = rstd.g(0, a, b)
            act_(r, P(4, 0, b - a), AF.Ln, scale=1.0 / D, bias=EPS)
            act_(r, r, AF.Exp, scale=-0.5)
        for g in range(8):
            o = hb.g(g, c0, c1) if outf is None else outf(g)
            stt('dve', o, xt.g(g, c0, c1), col(wname, g), rstd.g(0, c0, c1), ALU.mult, ALU.mult)

    def proj(sl, off, M, bank, rhs_c0=2, N=512, kg=8, rhs=None):
        src = hb if rhs is None else rhs
        for k in range(kg):
            mm(P(bank, 0, N, p=M), V(sl.t[:, k, off:off + M], sl.all().keys), src.g(k, rhs_c0, rhs_c0 + N), k == 0, k == kg - 1)

    def sigm_from(out, in_, t, scale=1.0, nbias=0.0):
        if isinstance(nbias, V):
            act_(t, in_, AF.Exp, scale=-scale, bias=nbias)
        else:
            act_(t, in_, AF.Relu, scale=scale, bias=40.0 - nbias)
            act_(t, t, AF.Exp, scale=-1.0, bias=40.0)
        act_(t, t, AF.Ln, scale=1.0, bias=1.0)
        act_(out, t, AF.Exp, scale=-1.0)

    def T_(i, c0=None, c1=None, p=128):
        return V(tmp[i].t[0:p, 0, c0:c1], [(tmp[i].name, 0)])

    def record(fn):
        saved = S.ops
        S.ops = []
        fn()
        out = S.ops
        S.ops = saved
        return out

    def interleave(lists):
        idx = [0] * len(lists)
        while True:
            live = [i for i in range(len(lists)) if idx[i] < len(lists[i])]
            if not live:
                break
            i = min(live, key=lambda q: idx[q] / len(lists[q]))
            S.ops.append(lists[i][idx[i]])
            idx[i] += 1

    def pipeline(n, stages):
        ns = len(stages)
        for t in range(n + ns - 1):
            for si in range(ns - 1, -1, -1):
                u = t - si
                if 0 <= u < n:
                    stages[si](u)

    subcnt = [0, 0]

    def gla_prep(h, dk, qv, kv, lgs, sc, rev, qscale, ib=8, ie=9, bufs=None, CL=32):
        p = dk
        qd, ki, ktl, lam = bufs[:4]
        nck = 512 // CL
        bs = T_(ib, 0, 512, p)
        E = T_(ie, 0, 512, p)
        if CL == 32:
            rm = rmB if rev else rmF
            if rev:
                scan(V(tmp[ib].t[0:p, 0, 511::-1], tmp[ib].all().keys), V(rm.t[0:p, 0, 511::-1], rm.all().keys),
                     V(lgs.ap[:, ::-1], lgs.keys), 0.0)
            else:
                scan(bs, V(rm.t[0:p, 0, :], rm.all().keys), lgs, 0.0)
        else:
            ones = V(onesb.t[0:p, 0, 0:CL], onesb.all().keys)
            for sg in range(nck):
                a, b = sg * CL, (sg + 1) * CL
                if rev:
                    scan(V(tmp[ib].t[0:p, 0, a:b][:, ::-1], tmp[ib].all().keys), ones, V(lgs.ap[:, a:b][:, ::-1], lgs.keys), 0.0)
                else:
                    scan(V(tmp[ib].t[0:p, 0, a:b], tmp[ib].all().keys), ones, V(lgs.ap[:, a:b], lgs.keys), 0.0)
        act_(E, bs, AF.Exp, scale=sc)
        stt('dve', qd.gp(h, p), qv, qscale, E, ALU.mult, ALU.mult)
        act_(E, bs, AF.Exp, scale=-sc)
        tt('pool', ki.gp(h, p), kv, E, ALU.mult)
        lc = 0 if rev else CL - 1
        bl = V(tmp[ib].t[0:p, 0, lc:512:CL], tmp[ib].all().keys)
        act_(V(lam.t[0:p, h, 0:nck], [(lam.name, h)]), bl, AF.Exp, scale=sc)
        blb = V(tmp[ib].t[0:p, 0, lc:512:CL].unsqueeze(2).to_broadcast([p, nck, CL]), tmp[ib].all().keys)
        E3 = V(tmp[ie].t[0:p, 0, 0:512].rearrange("p (c j) -> p c j", j=CL), tmp[ie].all().keys)
        bs3 = V(tmp[ib].t[0:p, 0, 0:512].rearrange("p (c j) -> p c j", j=CL), tmp[ib].all().keys)
        tt('dve', E3, blb, bs3, ALU.subtract)
        act_(E, E, AF.Exp, scale=sc)
        tt('pool', ktl.gp(h, p), kv, E, ALU.mult)

    def gla_core(mi, dk, rev, ob, accumulate, bufs, CL=32):
        p = dk
        qd, ki, ktl, lam, Vt = bufs
        nch = 128 // CL
        if CL == 32:
            mk = maskB if rev else maskF
        else:
            mk = maskB1 if rev else maskF1
        for s in (range(3, -1, -1) if rev else range(4)):
            si = subcnt[mi]
            subcnt[mi] += 1
            km = ktm[si % 2]
            am = attM[si % 2]
            i0 = nch * si
            c0 = s * 128
            for h in range(4):
                tr(V(psT[:, h * 128:h * 128 + dk], [('psT', 0)]), ktl.gp(h, p, c0, c0 + 128),
                   V(identb.t[0:p, 0, 0:p], identb.all().keys))
            if nch == 1:
                cp('dve', V(km.t[:, :, 0:dk], km.all().keys),
                   V(psT[:, 0:512].rearrange("p (h d) -> p h d", h=4)[:, :, 0:dk], [('psT', 0)]))
            else:
                for h in range(4):
                    in0 = V(psT[:, h * 128:h * 128 + dk].unsqueeze(1).to_broadcast([128, 4, dk]), [('psT', 0)])
                    in1 = V(cmask.t[:, 0, 0:4].unsqueeze(2).to_broadcast([128, 4, dk]), cmask.all().keys)
                    outv = V(km.t[:, h, :].rearrange("p (c d) -> p c d", c=4)[:, :, 0:dk], [(km.name, h)])
                    tt('dve', outv, in0, in1, ALU.mult)
            for h in range(4):
                mm(P(5, h * 128, h * 128 + 128), ki.gp(h, p, c0, c0 + 128), qd.gp(h, p, c0, c0 + 128), True, True)
            mkb = V(mk.t[:, 0, :].unsqueeze(1).to_broadcast([128, 4, 128]), mk.all().keys)
            tt('dve', V(am.t[:, 0, :].rearrange("p (h i) -> p h i", h=4), am.all().keys),
               V(ps[5][:, :].rearrange("p (h i) -> p h i", h=4), [('ps', 5)]), mkb, ALU.mult)
            corder = list(range(nch - 1, -1, -1) if rev else range(nch))
            KB = (6, 4, 6, 4)

            def kv_mm(ci):
                c = corder[ci]
                for h in range(4):
                    kmv = V(km.t[:, h, c * 128:c * 128 + dk], [(km.name, h)])
                    mm(P(KB[ci], h * 128, h * 128 + 128, p=dk), kmv, V(Vt.t[:, s, h * 128:(h + 1) * 128], [(Vt.name, s)]), True, True)
            for ci in range(min(2, nch)):
                kv_mm(ci)
            for ci, c in enumerate(corder):
                if ci in (1, 2) and ci + 1 < nch:
                    kv_mm(ci + 1)
                gch = s * nch + c
                for h in range(4):
                    sv = V(Scur[mi].t[0:p, h, :], [(Scur[mi].name, h)])
                    stt('dve', sv, sv, V(lam.t[0:p, h, gch:gch + 1], [(lam.name, h)]), P(KB[ci], h * 128, h * 128 + 128, p=dk), ALU.mult, ALU.add)
                slot = (i0 + ci + 1) % 5
                cp('act', V(Sbf[mi].t[0:p, :, slot * 128:(slot + 1) * 128], Sbf[mi].all().keys),
                   V(Scur[mi].t[0:p, :, :], Scur[mi].all().keys))
            mm(P(3), V(zerob.t[:, 0, 0:128], zerob.all().keys), zerob.all(), True, True)
            for h in range(4):
                mm(P(3, h * 128, h * 128 + 128), V(Vt.t[:, s, h * 128:(h + 1) * 128], [(Vt.name, s)]),
                   V(am.t[:, 0, h * 128:(h + 1) * 128], am.all().keys), False, True)
            for ci, c in enumerate(corder):
                slot = (i0 + ci) % 5
                for h in range(4):
                    mm(P(3, h * 128 + c * CL, h * 128 + c * CL + CL),
                       V(Sbf[mi].t[0:p, h, slot * 128:(slot + 1) * 128], [(Sbf[mi].name, h)]),
                       qd.gp(h, p, c0 + c * CL, c0 + c * CL + CL), False, True)
            outv = V(ob.t[:, :, c0:c0 + 128], ob.all().keys)
            inv = V(ps[3][:, :].rearrange("p (h i) -> p h i", h=4), [('ps', 3)])
            if accumulate:
                tt('dve', outv, inv, outv, ALU.add)
            else:
                cp('act', outv, inv)

    def v_proj(sl, vt, banks=(0,)):
        for s in range(4):
            bk = banks[s % len(banks)]
            for k in range(8):
                mm(P(bk), hb.g(k, 2 + 128 * s, 2 + 128 * s + 128), sl.g(k), k == 0, k == 7)
            cp('act', V(vt.t[:, s, :], [(vt.name, s)]), P(bk))

    def mixer_sweep(l, rev):
        d = 1 if rev else 0
        for mi in range(2):
            memset('pool', Scur[mi].all(), 0.0)
            memset('pool', Sbf[mi].all(), 0.0)
            subcnt[mi] = 0
        memset('pool', hst.all(), 0.0)
        order = list(range(NT - 1, -1, -1) if rev else range(NT))

        def ldx(nn):
            dma('q_pool', xt.all(), V(xA[:, nn * TT:nn * TT + 516].rearrange("(g p) c -> p g c", p=128), ['xA']), key='ldx')
        for oi, n in enumerate(order):
            if oi == 0:
                ldx(n)
            rmsnorm(('nmix', l), 0, 516)
            if oi + 1 < len(order):
                ldx(order[oi + 1])
            if not rev:
                for b in range(3):
                    dma('q_pool', obuf[b].all(), V(obwd[b * 512:(b + 1) * 512, n * TT:(n + 1) * TT].rearrange("(g p) c -> p g c", p=128), ['obwd']), key=('ldo', b))
            bufA = (qd, ki, ktl, lam, Vt)
            bufB = (qdB, kiB, ktlB, lamB, VtB)
            assert n_mix >= 3
            v_proj(load_w(wb_in[l], C_IA, 512), Vt, banks=(0, 1))
            slq = load_w(wb_in[l], C_QA, 512)
            slz = load_w(wb_in[l], C_ZB if rev else C_ZF, 512)

            def a0(h):
                proj(slq, h * 128, 128, 2 * (h % 2))
                proj(slz, h * 128, 128, 2 * (h % 2) + 1)

            def a1(h):
                o = 5 * (h % 2)
                bq, bz = 2 * (h % 2), 2 * (h % 2) + 1
                act_(T_(o, 0, 512), P(bq), AF.Sigmoid)
                tt('dve', T_(o, 0, 512), T_(o, 0, 512), P(bq), ALU.mult)
                act_(T_(o + 1, 0, 512), P(bz), AF.Sigmoid)
                ts('dve', T_(o + 1, 0, 512), T_(o + 1, 0, 512), col(('omlb', l, d), h), col(('lb', l, d), h), ALU.mult, ALU.add)
                act_(T_(o + 2, 0, 512), T_(o + 1, 0, 512), AF.Ln)
                ts('pool', T_(o + 1, 0, 512), T_(o + 1, 0, 512), -1.0, 1.0, ALU.mult, ALU.add)

            def a2(h):
                o = 5 * (h % 2)
                gla_prep(h, 128, T_(o, 0, 512), T_(o + 1, 0, 512), T_(o + 2, 0, 512), 1.0, rev, 128 ** -0.5, ib=o + 3, ie=o + 4, bufs=bufA)
            pipeline(4, [a0, a1, a2])

            def b_prep():
                v_proj(load_w(wb_in[l], C_VB, 512), VtB, banks=(0, 1))
                slqk = load_w(wb_in[l], C_QB, 512)
                sllr = load_w(wb_in[l], C_LRF, 32)
                proj(sllr, 16 * d, 16, 0)
                cp('act', V(lrb.t[0:16, 0, :], lrb.all().keys), P(0, 0, 512, p=16))

                def b0(h):
                    proj(slqk, h * 64, 64, 0)
                    proj(slqk, 256 + h * 64, 64, 1)
                    mm(P(2, 0, 512, p=64), V(upw.t[0:16, d, h * 64:(h + 1) * 64], upw.all().keys), V(lrb.t[0:16, 0, :], lrb.all().keys), True, True)

                def b1(h):
                    o = 5 * (h % 2)
                    act_(T_(o + 2, 0, 512, 64), P(2, 0, 512, p=64), AF.Exp, scale=-1.0, bias=col(('nupb', l, d), h, p=64))
                    act_(T_(o + 2, 0, 512, 64), T_(o + 2, 0, 512, 64), AF.Ln, scale=1.0, bias=1.0)
                    cp('act', T_(o, 0, 512, 64), P(0, 0, 512, p=64))
                    cp('dve', T_(o + 1, 0, 512, 64), P(1, 0, 512, p=64))

                def b2(h):
                    o = 5 * (h % 2)
                    gla_prep(h, 64, T_(o, 0, 512, 64), T_(o + 1, 0, 512, 64), T_(o + 2, 0, 512, 64), -1.0 / 16.0, rev, 64 ** -0.5, ib=o + 3, ie=o + 4, bufs=bufB, CL=BCL)
                pipeline(4, [b0, b1, b2])
            a_core_ops = record(lambda: gla_core(0, 128, rev, obuf[0], not rev, bufA))
            b_prep_ops = record(b_prep)
            if INTERLEAVE:
                interleave([a_core_ops, b_prep_ops])
            else:
                S.ops.extend(a_core_ops)
                S.ops.extend(b_prep_ops)

            def c_mixer():
                slc = load_w(wb_in[l], C_XC, 512)
                for g in range(4):
                    proj(slc, g * 128, 128, 0, rhs_c0=0, N=512)
                    cp('act', T_(7, 0, 512), P(0))
                    proj(slc, g * 128, 128, 1, rhs_c0=512, N=4)
                    cp('dve', T_(7, 512, 516), P(1, 0, 4))
                    xc = T_(6, 0, 512)
                    act_(xc, T_(7, 0, 512), AF.Identity, scale=col(('ccw', l, 0), g), bias=col(('ccb', l), g))
                    for j in range(1, 4):
                        stt('dve', xc, T_(7, j, j + 512), col(('ccw', l, j), g), xc, ALU.mult, ALU.add)
                    cp('act', xcb.g(g), xc)
                    mm(P(0), wbd.g(d * 8 + g), xcb.g(g), True, True)
                    mm(P(1), wbd.g(d * 8 + 4 + g), xcb.g(g), True, True)
                    sigm_from(T_(0, 0, 512), P(0), T_(1, 0, 512), nbias=col(('nba', l, d), g))
                    sigm_from(T_(2, 0, 512), P(1), T_(1, 0, 512), nbias=col(('nbx', l, d), g))
                    act_(T_(4, 0, 512), T_(0, 0, 512), AF.Exp, scale=col(('clam', l, d), g))
                    act_(T_(5, 0, 512), T_(0, 0, 512), AF.Exp, scale=col(('clam2', l, d), g))
                    ts('pool', T_(5, 0, 512), T_(5, 0, 512), -1.0, 1.0, ALU.mult, ALU.add)
                    act_(T_(5, 0, 512), T_(5, 0, 512), AF.Ln)
                    act_(T_(5, 0, 512), T_(5, 0, 512), AF.Exp, scale=0.5)
                    tt('pool', T_(2, 0, 512), T_(2, 0, 512), xc, ALU.mult)
                    tt('dve', T_(2, 0, 512), T_(2, 0, 512), T_(5, 0, 512), ALU.mult)
                    hs = V(hst.t[:, g, d:d + 1], [('hst', g)])
                    if rev:
                        scan(V(tmp[1].t[:, 0, 511::-1], tmp[1].all().keys),
                             V(tmp[4].t[:, 0, 511::-1], tmp[4].all().keys), V(tmp[2].t[:, 0, 511::-1], tmp[2].all().keys), hs)
                        cp('dve', hs, T_(1, 0, 1))
                        cp('act', obuf[2].g(g), T_(1, 0, 512))
                    else:
                        scan(T_(1, 0, 512), T_(4, 0, 512), T_(2, 0, 512), hs)
                        cp('dve', hs, T_(1, 511, 512))
                        tt('pool', obuf[2].g(g), T_(1, 0, 512), obuf[2].g(g), ALU.add)


            b_core_ops = record(lambda: gla_core(1, 64, rev, obuf[1], not rev, bufB, CL=BCL))
            c_ops = record(c_mixer)
            if INTERLEAVE:
                interleave([b_core_ops, c_ops])
            else:
                S.ops.extend(b_core_ops)
                S.ops.extend(c_ops)
            if rev:
                for b in range(3):
                    dma('q_pool', V(obwd[b * 512:(b + 1) * 512, n * TT:(n + 1) * TT].rearrange("(g p) c -> p g c", p=128), ['obwd']),
                        obuf[b].all(), key=('sto', b))
                continue
            BK3 = ((0, 1), (2, 3), (5, 6))
            for b in range(2):
                if n_mix < b + 1:
                    memset('pool', obf[b].all(), 0.0)
                else:
                    act_(V(sqb.t[:, 4 * b:4 * b + 4, 0:512], [('sqb', 4 * b + q) for q in range(4)]), obuf[b].all(), AF.Square)
            slog = {}
            pcfg = ((C_OGA, 'hnw'), (C_OGB, 'gnw'))

            def pa0(u):
                b, h = u // 4, u % 4
                mm(P(BK3[u % 3][0]), onesb.all(), sqb.g(4 * b + h, 0, 512), True, True)

            def pa1(u):
                ta = T_(8 + u % 3, 0, 512)
                act_(ta, P(BK3[u % 3][0]), AF.Ln, scale=1.0 / 128.0, bias=EPS)
                act_(ta, ta, AF.Exp, scale=-0.5)

            def pa2(u):
                b, h = u // 4, u % 4
                stt('dve', T_(u, 0, 512), obuf[b].g(h), col((pcfg[b][1], l)), T_(8 + u % 3, 0, 512), ALU.mult, ALU.mult)
            pipeline(8, [pa0, pa1, pa2])

            def pb0(u):
                b, h = u // 4, u % 4
                if h == 0:
                    slog[b] = load_w(wb_in[l], pcfg[b][0], 512)
                proj(slog[b], h * 128, 128, BK3[u % 3][1])

            def pb1(u):
                act_(T_(8 + u % 3, 0, 512), P(BK3[u % 3][1]), AF.Silu)

            def pb2(u):
                b, h = u // 4, u % 4
                tt('dve', obf[b].g(h), T_(u, 0, 512), T_(8 + u % 3, 0, 512), ALU.mult)
            pipeline(8, [pb0, pb1, pb2])
            if n_mix >= 3:
                sly = load_w(wb_in[l], C_YC, 512)

                def y0(g):
                    proj(sly, g * 128, 128, g % 2)

                def y1(g):
                    act_(T_(g % 3, 0, 512), P(g % 2), AF.Gelu_apprx_tanh)

                def y2(g):
                    tt('dve', obf[2].g(g), T_(g % 3, 0, 512), obuf[2].g(g), ALU.mult)
                pipeline(4, [y0, y1, y2])
            else:
                memset('pool', obf[2].all(), 0.0)
            BK = ((0, 1), (2, 3), (5, 6))
            wl = {}

            def mg0(u):
                gc, br, gg = u // 12, (u % 12) // 4, u % 4
                if gg == 0:
                    wl[(gc, br)] = (load_w(wb_in[l], C_GA + br * 1024 + gc * 512, 512),
                                    load_w(wb_br[l, br * 512:(br + 1) * 512, :], gc * 512, 512, kg=4))
                slg, slb = wl[(gc, br)]
                proj(slg, gg * 128, 128, BK[u % 3][0])
                proj(slb, gg * 128, 128, BK[u % 3][1], rhs_c0=0, kg=4, rhs=obf[br])

            def mg1(u):
                act_(T_(u % 3, 0, 512), P(BK[u % 3][0]), AF.Sigmoid)

            def mg2(u):
                gc, br, gg = u // 12, (u % 12) // 4, u % 4
                g = gc * 4 + gg
                macc = T_(3 + g, 0, 512)
                sg = T_(u % 3, 0, 512)
                if br == 0:
                    tt('dve', macc, sg, P(BK[u % 3][1]), ALU.mult)
                else:
                    tt('dve', sg, sg, P(BK[u % 3][1]), ALU.mult)
                    tt('pool', macc, macc, sg, ALU.add)
                if br == 2:
                    cp('act', mbf.g(g), macc)
            pipeline(24, [mg0, mg1, mg2])
            wo = {}

            def wo0(u):
                if u % 4 == 0:
                    wo[u // 4] = load_w(wb_out[l], (u // 4) * 512, 512)
                dma('q_pool', T_(3 + u, 0, 512), V(xA[u * 128:(u + 1) * 128, n * TT + 2:n * TT + 2 + TT], ['xA']), key=('ldr', u % 4))
                proj(wo[u // 4], (u % 4) * 128, 128, u % 2, rhs_c0=0, rhs=mbf)

            def wo1(u):
                tt('dve', T_(3 + u, 0, 512), P(u % 2), T_(3 + u, 0, 512), ALU.add)
                dma('q_pool', V(xB[u * 128:(u + 1) * 128, n * TT + 2:n * TT + 2 + TT], ['xB']), T_(3 + u, 0, 512), key=('str', u % 4))
            pipeline(8, [wo0, wo1])

    def gelu_mul(outv, cg, cv, t1):
        act_(t1, cg, AF.Square)
        ts('dve', t1, t1, 0.044715, 1.0, ALU.mult, ALU.add)
        tt('dve', t1, t1, cg, ALU.mult)
        act_(t1, t1, AF.Relu, scale=1.5957691216057308, bias=40.0)
        act_(t1, t1, AF.Exp, scale=-1.0, bias=40.0)
        act_(t1, t1, AF.Ln, scale=1.0, bias=1.0)
        act_(t1, t1, AF.Exp, scale=-1.0)
        tt('pool', t1, t1, cg, ALU.mult)
        tt('dve', outv, t1, cv, ALU.mult)

    def FT(k, c0=None, c1=None):
        if k < 10:
            return T_(k, c0, c1)
        return obuf[(k - 10) // 4].g((k - 10) % 4, c0, c1)

    def ffn_sweep(l):
        def wlen(ww):
            return min(510, T - 510 * ww) + 2

        def ldxw(ww):
            W_ = wlen(ww)
            dma('q_pool', V(xt.t[:, :, 0:W_], xt.all().keys), V(xB[:, 510 * ww + 1:510 * ww + 1 + W_].rearrange("(g p) c -> p g c", p=128), ['xB']), key='ldx')
        for w in range(NWIN):
            c0 = 510 * w + 1
            nout = min(510, T - 510 * w)
            WL = nout + 2
            if w == 0:
                ldxw(0)
            rmsnorm(('nffn', l), 0, WL)
            if w + 1 < NWIN:
                ldxw(w + 1)
            wl = {}
            FB = ((0, 1), (2, 3), (5, 6))

            def f0(j):
                jb, gg = j // 4, j % 4
                if gg == 0:
                    ng = 4 if jb < 5 else 2
                    wl[jb] = (load_w(wb_up[l], jb * 512, ng * 128), load_w(wb_up[l], DFF + jb * 512, ng * 128))
                proj(wl[jb][0], gg * 128, 128, FB[j % 3][0], rhs_c0=0, N=WL)
                proj(wl[jb][1], gg * 128, 128, FB[j % 3][1], rhs_c0=0, N=WL)

            def f1(j):
                for q, fi in ((0, j), (1, 22 + j)):
                    bk = FB[j % 3][q]
                    C = FT(4 + 2 * (j % 4) + q, 1, WL - 1)
                    act_(C, P(bk, 0, WL - 2), AF.Identity, scale=col(('fcw', l, 0), fi), bias=col(('fcb', l), fi))
                    stt('dve', C, P(bk, 1, WL - 1), col(('fcw', l, 1), fi), C, ALU.mult, ALU.add)
                    stt('dve', C, P(bk, 2, WL), col(('fcw', l, 2), fi), C, ALU.mult, ALU.add)

            def f2(j):
                cg = FT(4 + 2 * (j % 4), 1, WL - 1)
                t1 = FT(12 + j % 3, 1, WL - 1)
                act_(t1, cg, AF.Gelu_apprx_tanh)

            def f3(j):
                cv = FT(4 + 2 * (j % 4) + 1, 1, WL - 1)
                t1 = FT(12 + j % 3, 1, WL - 1)
                tt('dve', actg(j, 0, nout), t1, cv, ALU.mult)
            pipeline(22, [f0, f1, f2, f3])
            for oc in range(2):
                for kb in range(3):
                    nk = 8 if kb < 2 else 6
                    sd = load_w(wb_dn[l, kb * 1024:kb * 1024 + nk * 128, :], oc * 512, 512, kg=nk)
                    for gg in range(4):
                        for kk in range(nk):
                            j = kb * 8 + kk
                            mm(P(gg, 0, nout), V(sd.t[:, kk, gg * 128:(gg + 1) * 128], sd.all().keys), actg(j, 0, nout), j == 0, j == 21)
                for gg in range(4):
                    g = oc * 4 + gg
                    r = lambda a, b: FT(15 + gg, a, b)
                    dma('q_pool', r(0, nout), V(xB[g * 128:(g + 1) * 128, c0 + 1:c0 + 1 + nout], ['xB']), key=('ldr', gg))
                    tt('dve', r(0, nout), P(gg, 0, nout), r(0, nout), ALU.add)
                    dma('q_pool', V(xA[g * 128:(g + 1) * 128, 510 * w + 2:510 * w + 2 + nout], ['xA']), r(0, nout), key=('str', gg))

    for l in range(L):
        dma('sp', upwf.all(), V(gla_up_w[l].rearrange("d r c -> r d c"), []), key='upw')
        cp('dve', upw.all(), upwf.all())
        for d in range(2):
            for ax, wsrc in ((0, rglru_wa), (1, rglru_wx)):
                stg = V(tmp[10].t[:, 0, 0:512].rearrange("p (g c) -> p g c", g=4), tmp[10].all().keys)
                memset('pool', stg, 0.0)
                for g in range(4):
                    for hh in range(2):
                        dma('sp', V(tmp[10].t[hh * 64:(hh + 1) * 64, 0, g * 128 + hh * 64:g * 128 + hh * 64 + 64], tmp[10].all().keys),
                            V(wsrc[l, d, 2 * g + hh], []), key=('wbd', 2 * g + hh))
                cp('dve', V(wbd.t[:, d * 8 + ax * 4:d * 8 + ax * 4 + 4, :], [('wbd', d * 8 + ax * 4 + q) for q in range(4)]), stg)
        mixer_sweep(l, True)
        mixer_sweep(l, False)
        ffn_sweep(l)

    for n in range(NT):
        dma('q_pool', V(xt.t[:, :, 0:512], xt.all().keys), V(xA[:, n * TT + 2:n * TT + 2 + TT].rearrange("(g p) c -> p g c", p=128), ['xA']), key='ldx')
        rmsnorm('fnw', 0, 512, outf=lambda g: obuf[1 + g // 4].g(g % 4))
        for s in range(4):
            for half in range(2):
                pb = 5 + half
                for gg in range(4):
                    g = half * 4 + gg
                    tr(P(pb, gg * 128, gg * 128 + 128), obuf[1 + g // 4].g(g % 4, s * 128, s * 128 + 128), identf.all())
                cp('act' if half == 0 else 'dve', V(obuf[0].t[:, 2 * (s % 2) + half, :], [('obuf0', 2 * (s % 2) + half)]), P(pb))
            dma('sp', V(y_out[n * TT + s * 128:n * TT + (s + 1) * 128, :], ['y']),
                V(obuf[0].t[:, 2 * (s % 2):2 * (s % 2) + 2, :].rearrange("p g c -> p (g c)"), [('obuf0', 2 * (s % 2)), ('obuf0', 2 * (s % 2) + 1)]), key=('sty', s % 2))
    st = S.emit()
    return nc, st


_CACHE = {}
WNAMES = ["norm_mix_w", "w_in", "hgrn_lb_logits", "hgrn_norm_w", "gla_up_w", "gla_up_b", "gla_norm_w", "c_conv_w",
          "c_conv_b", "rglru_wa", "rglru_ba", "rglru_wx", "rglru_bx", "rglru_lam", "w_branch", "w_out", "norm_ffn_w",
          "ffn_up", "ffn_conv_w", "ffn_conv_b", "ffn_down", "final_norm_w"]


def kernel(x_prompt, x_sample, **w):
    x_prompt = np.asarray(x_prompt, np.float32)
    x_sample = np.asarray(x_sample, np.float32)
    T = x_prompt.shape[1]
    L = np.asarray(w["w_in"]).shape[0]
    seqs = [x_prompt[i] for i in range(x_prompt.shape[0])] + [x_sample[i] for i in range(x_sample.shape[0])]
    key = (T, L)
    if key not in _CACHE:
        _CACHE[key] = build(T, L)[0]
    nc = _CACHE[key]
    wd = {k: np.ascontiguousarray(np.asarray(w[k], np.float32)) for k in WNAMES}
    in_maps = []
    for c in range(8):
        m = dict(wd)
        m["x"] = np.ascontiguousarray(seqs[c]) if c < len(seqs) else np.zeros((T, D), np.float32)
        in_maps.append(m)
    res = run_bass_kernel_spmd(nc, in_maps, core_ids=list(range(8)))
    ys = [np.asarray(res.results[c]["y"], np.float32) for c in range(len(seqs))]
    nb = x_prompt.shape[0]
    return (np.stack(ys[:nb], 0), np.stack(ys[nb:], 0))
```
